# Optimizing a Trainium2 kernel written in Bass

```python
import jax, jax.numpy as jnp
from jax import lax
import numpy as np

D_MODEL = 1024
BATCH = 2
SEQ = 16384
DEPTH = 2

N_A_LAYERS = DEPTH // 2
N_B_LAYERS = DEPTH - N_A_LAYERS

LRU_WIDTH = D_MODEL
LRU_BLOCKS = 8
LRU_BLOCK_W = LRU_WIDTH // LRU_BLOCKS
CONV_WIDTH = 4
LRU_C = 8.0

HEAD_DIM = 128
Q_HEADS = D_MODEL // HEAD_DIM
KV_HEADS = 2
Q_PER_KV = Q_HEADS // KV_HEADS
DILATION_GROUPS = ((128, 1), (512, 4), (2048, 16))
N_GROUPS = len(DILATION_GROUPS)
ATT_BLOCK = 128
ROPE_DIM = HEAD_DIM // 4
ROPE_THETA = 500000.0

FFN_HIDDEN = -(-8 * D_MODEL // (3 * 256)) * 256
EPS = 1e-6
NEG_INF = -1e30

kernel_name = "yoco_rglru_dilated_swa_hybrid"


def rms_norm(x, g):
    x32 = x.astype(jnp.float32)
    y = x32 * lax.rsqrt(jnp.mean(x32 * x32, axis=-1, keepdims=True) + EPS)
    return (y * g.astype(jnp.float32)).astype(x.dtype)


def rope_tables(positions):
    inv_freq = ROPE_THETA ** (-jnp.arange(0, ROPE_DIM, 2, dtype=jnp.float32) / ROPE_DIM)
    ang = positions.astype(jnp.float32)[..., None] * inv_freq
    return jnp.cos(ang), jnp.sin(ang)


def apply_partial_rope(x, cos, sin):
    extra = x.ndim - 3
    shp = cos.shape[:2] + (1,) * extra + cos.shape[-1:]
    c, s = cos.reshape(shp), sin.reshape(shp)
    half = ROPE_DIM // 2
    xr = x[..., :ROPE_DIM].astype(jnp.float32)
    x1, x2 = xr[..., :half], xr[..., half:]
    rot = jnp.concatenate([x1 * c - x2 * s, x2 * c + x1 * s], axis=-1)
    return jnp.concatenate([rot.astype(x.dtype), x[..., ROPE_DIM:]], axis=-1)


def causal_depthwise_conv(x, w, b):
    y = lax.conv_general_dilated(
        x, w[:, None, :].astype(x.dtype), window_strides=(1,),
        padding=((CONV_WIDTH - 1, 0),), dimension_numbers=('NWC', 'WIO', 'NWC'),
        feature_group_count=x.shape[-1])
    return y + b.astype(x.dtype)


def rg_lru(x, w_a, b_a, w_x, b_x, lam):
    B, S, C = x.shape
    x32 = x.astype(jnp.float32)
    xb = x32.reshape(B, S, LRU_BLOCKS, LRU_BLOCK_W)
    r_gate = jax.nn.sigmoid(jnp.einsum('bshi,hij->bshj', xb, w_a.astype(jnp.float32)).reshape(B, S, C) + b_a.astype(jnp.float32))
    i_gate = jax.nn.sigmoid(jnp.einsum('bshi,hij->bshj', xb, w_x.astype(jnp.float32)).reshape(B, S, C) + b_x.astype(jnp.float32))
    log_a = -LRU_C * r_gate * jax.nn.softplus(-lam.astype(jnp.float32))
    a = jnp.exp(log_a)
    u = jnp.sqrt(-jnp.expm1(2.0 * log_a)) * (i_gate * x32)

    def combine(left, right):
        a1, b1 = left
        a2, b2 = right
        return a1 * a2, a2 * b1 + b2

    _, h = lax.associative_scan(combine, (a, u), axis=1)
    return h.astype(x.dtype)


def recurrent_mixer(h, w_in, conv_w, conv_b, ga_w, ga_b, gx_w, gx_b, lam, w_out):
    proj = h @ w_in
    y_branch, x_branch = jnp.split(proj, 2, axis=-1)
    gate = jax.nn.gelu(y_branch, approximate=True)
    xc = causal_depthwise_conv(x_branch, conv_w, conv_b)
    hr = rg_lru(xc, ga_w, ga_b, gx_w, gx_b, lam)
    return (gate * hr) @ w_out


def swiglu(h, w_in, w_out):
    g, u = jnp.split(h @ w_in, 2, axis=-1)
    return (jax.nn.silu(g) * u) @ w_out


def shared_kv(h, kv_norm, w_kv, k_norm, cos, sin):
    B, S, _ = h.shape
    hn = rms_norm(h, kv_norm)
    kv = (hn @ w_kv).reshape(B, S, N_GROUPS, 2, KV_HEADS, HEAD_DIM)
    k, v = kv[:, :, :, 0], kv[:, :, :, 1]
    k = rms_norm(k, k_norm[:, None, :])
    k = apply_partial_rope(k, cos, sin)
    return k, v


def dilated_window_attention(q, k, v, window, dilation):
    B, S = q.shape[:2]
    n_keys = window // dilation
    span = dilation * ATT_BLOCK
    S_pad = -(-S // span) * span
    L = S_pad // dilation
    nb = L // ATT_BLOCK
    pad = S_pad - S

    def to_phase(t):
        t = jnp.pad(t, [(0, 0), (0, pad)] + [(0, 0)] * (t.ndim - 2))
        t = t.reshape((B, L, dilation) + t.shape[2:])
        t = jnp.moveaxis(t, 2, 1)
        return t.reshape((B, dilation, nb, ATT_BLOCK) + t.shape[3:])

    def with_prev(t):
        prev = jnp.pad(t, [(0, 0), (0, 0), (1, 0)] + [(0, 0)] * (t.ndim - 3))[:, :, :-1]
        return jnp.concatenate([prev, t], axis=3)

    qp = to_phase(q)
    kb = with_prev(to_phase(k))
    vb = with_prev(to_phase(v))
    s = jnp.einsum('bpnqhgd,bpnkhd->bpnhgqk', qp, kb).astype(jnp.float32) * (HEAD_DIM ** -0.5)
    q_idx = jnp.arange(ATT_BLOCK)[:, None] + ATT_BLOCK
    k_idx = jnp.arange(2 * ATT_BLOCK)[None, :]
    dist = q_idx - k_idx
    band = (dist >= 0) & (dist <= n_keys)
    has_prev = (jnp.arange(nb)[:, None, None] > 0) | (k_idx >= ATT_BLOCK)[None]
    mask = band[None] & has_prev
    s = jnp.where(mask[None, None, :, None, None], s, NEG_INF)
    m = jnp.max(s, axis=-1, keepdims=True)
    p = jnp.exp(s - m)
    den = jnp.sum(p, axis=-1, keepdims=True)
    o = jnp.einsum('bpnhgqk,bpnkhd->bpnqhgd', (p / den).astype(v.dtype), vb)
    lse = jnp.moveaxis((m + jnp.log(den))[..., 0], -1, 3)

    def from_phase(t):
        t = t.reshape((B, dilation, L) + t.shape[4:])
        t = jnp.moveaxis(t, 1, 2).reshape((B, S_pad) + t.shape[3:])
        return t[:, :S]

    return from_phase(o), from_phase(lse)


def dilated_attention_mixer(h, w_q, q_norm, w_o, k, v, cos, sin):
    B, S, _ = h.shape
    q = (h @ w_q).reshape(B, S, N_GROUPS, Q_HEADS, HEAD_DIM)
    q = rms_norm(q, q_norm[:, None, :])
    q = apply_partial_rope(q, cos, sin)
    q = q.reshape(B, S, N_GROUPS, KV_HEADS, Q_PER_KV, HEAD_DIM)
    outs, lses = [], []
    for g, (window, dilation) in enumerate(DILATION_GROUPS):
        o, l = dilated_window_attention(q[:, :, g], k[:, :, g], v[:, :, g], window, dilation)
        outs.append(o)
        lses.append(l)
    wts = jax.nn.softmax(jnp.stack(lses, axis=0), axis=0)
    o = jnp.sum(wts[..., None] * jnp.stack(outs, axis=0).astype(jnp.float32), axis=0).astype(h.dtype)
    return o.reshape(B, S, Q_HEADS * HEAD_DIM) @ w_o


def setup_inputs(seed: int = 0) -> dict:
    key = jax.random.key(seed)
    ks = jax.random.split(key, 32)

    def nrm(k, shape, fan_in):
        return jax.random.normal(k, shape, jnp.float32) * (fan_in ** -0.5)

    def gain(k, shape):
        return 1.0 + 0.05 * jax.random.normal(k, shape, jnp.float32)

    def bias(k, shape):
        return 0.01 * jax.random.normal(k, shape, jnp.float32)

    nA, nB = N_A_LAYERS, N_B_LAYERS
    a8 = jax.random.uniform(ks[9], (nA, LRU_WIDTH), jnp.float32, 0.9, 0.999)
    a0 = a8 ** (1.0 / LRU_C)
    lam = jnp.log(a0) - jnp.log1p(-a0)
    return {
        "x": jax.random.normal(ks[0], (BATCH, SEQ, D_MODEL), jnp.float32),
        "positions": jnp.broadcast_to(jnp.arange(SEQ, dtype=jnp.int32)[None, :], (BATCH, SEQ)),
        "a_norm": gain(ks[1], (nA, D_MODEL)),
        "a_w_in": nrm(ks[2], (nA, D_MODEL, 2 * LRU_WIDTH), D_MODEL),
        "a_conv_w": nrm(ks[3], (nA, CONV_WIDTH, LRU_WIDTH), CONV_WIDTH),
        "a_conv_b": bias(ks[4], (nA, LRU_WIDTH)),
        "a_gate_a_w": nrm(ks[5], (nA, LRU_BLOCKS, LRU_BLOCK_W, LRU_BLOCK_W), LRU_BLOCK_W),
        "a_gate_a_b": bias(ks[6], (nA, LRU_WIDTH)),
        "a_gate_x_w": nrm(ks[7], (nA, LRU_BLOCKS, LRU_BLOCK_W, LRU_BLOCK_W), LRU_BLOCK_W),
        "a_gate_x_b": bias(ks[8], (nA, LRU_WIDTH)),
        "a_lambda": lam,
        "a_w_out": nrm(ks[10], (nA, LRU_WIDTH, D_MODEL), LRU_WIDTH),
        "a_ffn_norm": gain(ks[11], (nA, D_MODEL)),
        "a_ffn_w_in": nrm(ks[12], (nA, D_MODEL, 2 * FFN_HIDDEN), D_MODEL),
        "a_ffn_w_out": nrm(ks[13], (nA, FFN_HIDDEN, D_MODEL), FFN_HIDDEN),
        "kv_norm": gain(ks[14], (D_MODEL,)),
        "w_kv": nrm(ks[15], (D_MODEL, N_GROUPS * 2 * KV_HEADS * HEAD_DIM), D_MODEL),
        "k_norm": gain(ks[16], (N_GROUPS, HEAD_DIM)),
        "b_norm": gain(ks[17], (nB, D_MODEL)),
        "b_w_q": nrm(ks[18], (nB, D_MODEL, N_GROUPS * Q_HEADS * HEAD_DIM), D_MODEL),
        "b_q_norm": gain(ks[19], (nB, N_GROUPS, HEAD_DIM)),
        "b_w_o": nrm(ks[20], (nB, Q_HEADS * HEAD_DIM, D_MODEL), Q_HEADS * HEAD_DIM),
        "b_ffn_norm": gain(ks[21], (nB, D_MODEL)),
        "b_ffn_w_in": nrm(ks[22], (nB, D_MODEL, 2 * FFN_HIDDEN), D_MODEL),
        "b_ffn_w_out": nrm(ks[23], (nB, FFN_HIDDEN, D_MODEL), FFN_HIDDEN),
    }


def reference(x, positions, a_norm, a_w_in, a_conv_w, a_conv_b, a_gate_a_w, a_gate_a_b,
              a_gate_x_w, a_gate_x_b, a_lambda, a_w_out, a_ffn_norm, a_ffn_w_in, a_ffn_w_out,
              kv_norm, w_kv, k_norm, b_norm, b_w_q, b_q_norm, b_w_o, b_ffn_norm, b_ffn_w_in,
              b_ffn_w_out):
    cos, sin = rope_tables(positions)
    h = x
    k = v = None
    for layer in range(DEPTH):
        if layer < N_A_LAYERS:
            i = layer
            h = h + recurrent_mixer(rms_norm(h, a_norm[i]), a_w_in[i], a_conv_w[i], a_conv_b[i],
                                    a_gate_a_w[i], a_gate_a_b[i], a_gate_x_w[i], a_gate_x_b[i],
                                    a_lambda[i], a_w_out[i])
            h = h + swiglu(rms_norm(h, a_ffn_norm[i]), a_ffn_w_in[i], a_ffn_w_out[i])
        else:
            if layer == N_A_LAYERS:
                k, v = shared_kv(h, kv_norm, w_kv, k_norm, cos, sin)
            j = layer - N_A_LAYERS
            h = h + dilated_attention_mixer(rms_norm(h, b_norm[j]), b_w_q[j], b_q_norm[j], b_w_o[j],
                                            k, v, cos, sin)
            h = h + swiglu(rms_norm(h, b_ffn_norm[j]), b_ffn_w_in[j], b_ffn_w_out[j])
    return h
```

```python
import math
import os
import numpy as np
import ml_dtypes
from contextlib import ExitStack
import concourse.bass as bass
import concourse.mybir as mybir
from concourse.bass_utils import run_bass_kernel_spmd

F32 = mybir.dt.float32
BF16 = mybir.dt.bfloat16
I32 = mybir.dt.int32
AF = mybir.ActivationFunctionType
ALU = mybir.AluOpType
AX = mybir.AxisListType

NCORES = 8
D = 1024
KC = 8
S_CORE = 4096
T = 512
NT = S_CORE // T
FFN = 2816
FC = FFN // 128
EPS = 1e-6
DIL = (1, 4, 16)
TWO_PI = 2.0 * math.pi
CW1 = 6.28125
CW2 = TWO_PI - CW1
SCALE = 128.0 ** -0.5
NSLAB = 4

PV_ANORM, PV_CONVW, PV_CONVB, PV_GAB, PV_GXB, PV_LAM, PV_AFFN, PV_KVN, PV_BN, PV_BFFN, PV_KN, PV_QN, PV_INVF, PV_SEL, PV_HB = \
    0, 8, 40, 48, 56, 64, 72, 80, 88, 96, 104, 107, 110, 111, 119
NPV = 120
PV_EPS, PV_ONE, PV_ZERO = 120, 121, 122
NPVT = 123


class Sched:
    ENGS = ('pe', 'act', 'dve', 'pool', 'sp')

    def __init__(self, nc, es, pfx=""):
        self.nc = nc
        self.es = es
        self.pfx = pfx
        self.q = {e: [] for e in self.ENGS}
        self.cnt = {e: 0 for e in self.ENGS}
        self.sem = {e: es.enter_context(nc.semaphore(pfx + "s_" + e)) for e in self.ENGS if e != 'sp'}
        self.dsem = {}
        self.dcnt = {}
        self.last_w = {}
        self.readers = {}
        self.seen = {e: {} for e in self.ENGS}

    def _dma_sem(self, key):
        if key not in self.dsem:
            self.dsem[key] = self.es.enter_context(self.nc.semaphore(self.pfx + "d_" + key))
            self.dcnt[key] = 0
        return self.dsem[key]

    def _deps(self, reads, writes):
        toks = set()
        for k in list(reads) + list(writes):
            t = self.last_w.get(k)
            if t is not None:
                toks.add(t)
        for k in writes:
            for t in self.readers.get(k, ()):
                toks.add(t)
        return toks

    def _record(self, tok, reads, writes):
        for k in reads:
            self.readers.setdefault(k, []).append(tok)
        for k in writes:
            self.last_w[k] = tok
            self.readers[k] = []

    def _waits(self, eng, toks):
        need = {}
        for t in toks:
            if t[0] == 'c':
                _, e, idx = t
                if e == eng and e == 'pe':
                    continue
                key = ('c', e)
                need[key] = max(need.get(key, 0), idx)
            else:
                _, k, n = t
                key = ('d', k)
                need[key] = max(need.get(key, 0), n)
        out = []
        for key, val in need.items():
            if self.seen[eng].get(key, 0) >= val:
                continue
            self.seen[eng][key] = val
            sem = self.sem[key[1]] if key[0] == 'c' else self.dsem[key[1]]
            out.append((sem, val))
        return out

    @staticmethod
    def _excl(reads, writes):
        ps = [k for k in reads if isinstance(k, tuple) and k[0] == 'ps']
        if ps:
            reads = [k for k in reads if not (isinstance(k, tuple) and k[0] == 'ps')]
            writes = list(writes) + ps
        return reads, writes

    def op(self, eng, fn, reads=(), writes=()):
        reads, writes = self._excl(reads, writes)
        toks = self._deps(reads, writes)
        waits = self._waits(eng, toks)
        self.cnt[eng] += 1
        tok = ('c', eng, self.cnt[eng])
        self.q[eng].append((waits, fn, self.sem[eng], 1))
        self._record(tok, reads, writes)
        return tok

    def dma(self, eng, key, fn, reads=(), writes=(), serialize=True, inc=16):
        sem = self._dma_sem(key)
        toks = self._deps(reads, writes)
        if serialize and self.dcnt[key] > 0:
            toks.add(('d', key, self.dcnt[key]))
        waits = self._waits(eng, toks)
        self.dcnt[key] += inc
        tok = ('d', key, self.dcnt[key])
        self.q[eng].append((waits, fn, sem, inc))
        self._record(tok, reads, writes)
        return tok

    def finish(self, eng='sp'):
        waits = []
        for k, n in self.dcnt.items():
            if n > 0:
                waits.append((self.dsem[k], n))
        for e in self.ENGS:
            if e != 'sp' and e != eng and self.cnt[e] > 0:
                waits.append((self.sem[e], self.cnt[e]))
        self.q[eng].append((waits, None, None, 0))

    def barrier_all(self, skip=()):
        for eng in self.ENGS:
            waits = []
            for k, n in self.dcnt.items():
                if k in skip:
                    continue
                if n > 0 and self.seen[eng].get(('d', k), 0) < n:
                    self.seen[eng][('d', k)] = n
                    waits.append((self.dsem[k], n))
            for e in self.ENGS:
                if e != 'sp' and e != eng and self.cnt[e] > 0 and self.seen[eng].get(('c', e), 0) < self.cnt[e]:
                    self.seen[eng][('c', e)] = self.cnt[e]
                    waits.append((self.sem[e], self.cnt[e]))
            self.q[eng].append((waits, None, None, 0))

    def emit(self):
        nc = self.nc
        q = self.q

        def run(engobj, lst):
            for waits, fn, sem, inc in lst:
                for s, v in waits:
                    engobj.wait_ge(s, v)
                if fn is not None:
                    ins = fn(engobj)
                    ins.then_inc(sem, inc)

        with nc.Block() as block:
            @block.tensor
            def _(e):
                run(e, q['pe'])

            @block.scalar
            def _(e):
                run(e, q['act'])

            @block.vector
            def _(e):
                run(e, q['dve'])

            @block.gpsimd
            def _(e):
                run(e, q['pool'])

            @block.sync
            def _(e):
                run(e, q['sp'])


class Bld:
    def __init__(self, nc, es, pfx="", sem_es=None, dram=None, wcache=None, S=None):
        self.nc = nc
        self.es = es
        self.pfx = pfx
        self.dram = dram if dram is not None else {}
        self.wcache = wcache if wcache is not None else {}
        self.fused = dram is not None
        self.S = S if S is not None else Sched(nc, sem_es if sem_es is not None else es, pfx)
        self.banks = [es.enter_context(nc.psum_tensor(pfx + "psb%d" % i, [128, 512], F32)) for i in range(8)]
        self.bi = 0
        self.slabs = [self.sb("slab%d" % i, [128, 4096], BF16) for i in range(NSLAB)]
        self.si = 0
        self.jobs = []
        self.ci = 0

    def sb(self, name, shape, dt):
        return self.es.enter_context(self.nc.sbuf_tensor(self.pfx + name, shape, dt))

    def bank(self):
        i = self.bi
        self.bi = (self.bi + 1) % 8
        return self.banks[i], ('ps', i)

    def dram_in(self, name, shape, dt):
        if name in self.dram:
            return self.dram[name]
        return self.nc.dram_tensor(name, list(shape), dt, kind="ExternalInput").ap()

    def dram_out(self, name, shape, dt):
        if name in self.dram:
            return self.dram[name]
        return self.nc.dram_tensor(name, list(shape), dt, kind="ExternalOutput").ap()

    def dram_tmp(self, name, shape, dt):
        return self.nc.dram_tensor(name, list(shape), dt).ap()

    def cload(self, dst, src, wkey, eng='sp'):
        k = 'c%d' % (self.ci % 4)
        self.ci += 1
        wk = wkey if isinstance(wkey, list) else [wkey]
        self.S.dma(eng, k, lambda e: e.dma_start(out=dst, in_=src), writes=wk)

    def job(self, loads, compute):
        self.jobs.append((loads, compute))

    def run_jobs(self, lookahead=2):
        S = self.S
        jobs = self.jobs
        self.jobs = []
        assigned = {}

        def do_load(j):
            loads, _ = jobs[j]
            if not loads:
                return
            si = self.si
            self.si = (self.si + 1) % NSLAB
            assigned[j] = si
            slab = self.slabs[si]
            for lf in loads:
                dst, src, skey = lf(slab)
                S.dma('sp', 'w%d' % si, (lambda dst, src: lambda e: e.dma_start(out=dst, in_=src))(dst, src),
                      reads=list(skey), writes=[('slab', si)])

        load_idx = [j for j in range(len(jobs)) if jobs[j][0]]
        ptr = 0
        for j in range(len(jobs)):
            while ptr < len(load_idx):
                ahead = sum(1 for x in load_idx[max(0, ptr - 8):ptr] if x > j)
                if load_idx[ptr] <= j or ahead < lookahead:
                    do_load(load_idx[ptr])
                    ptr += 1
                else:
                    break
            _, compute = jobs[j]
            if j in assigned:
                compute(self.slabs[assigned[j]], ('slab', assigned[j]))
            else:
                compute(None, None)


def emit_rmsnorm(b, xt, xkey, n, gcol, pv, hn, hnkey, ones, sq, rt, tag, inv_n=1.0 / D):
    S = b.S

    def compute(_s, _k):
        ps, pk = b.bank()
        for c in range(KC):
            S.op('act', (lambda c: lambda e: e.activation(out=sq[:, c % 2, 0:n], in_=xt[:, c, 0:n], func=AF.Square))(c),
                 reads=[(xkey, c)], writes=[('sq', c % 2)])
            S.op('pe', (lambda c: lambda e: e.matmul(ps[:, 0:n], lhsT=ones[:], rhs=sq[:, c % 2, 0:n], start=(c == 0), stop=(c == KC - 1)))(c),
                 reads=[('sq', c % 2), 'ones'], writes=[pk])
        S.op('act', lambda e: e.activation(out=rt[:, 0:n], in_=ps[:, 0:n], func=AF.Sqrt, scale=inv_n, bias=pv[:, PV_EPS:PV_EPS + 1]),
             reads=[pk, 'pv'], writes=['rt'])
        S.op('dve', lambda e: e.reciprocal(out=rt[:, 0:n], in_=rt[:, 0:n]), reads=['rt'], writes=['rt'])
        for c in range(KC):
            S.op('dve', (lambda c: lambda e: e.scalar_tensor_tensor(out=hn[:, c, 0:n], in0=xt[:, c, 0:n], scalar=pv[:, gcol + c:gcol + c + 1],
                                                                     in1=rt[:, 0:n], op0=ALU.mult, op1=ALU.mult))(c),
                 reads=[(xkey, c), 'rt', 'pv'], writes=[(hnkey, c)])
    b.job([], compute)


def emit_linear(b, wsrc, wkey, ncols_total, kchunks, rhs_fn, rhs_keys, handler, n, col_groups=None):
    S = b.S
    wv = wsrc.rearrange("(k p) n -> p k n", p=128)
    if col_groups is None:
        per = 4096 // kchunks // 128 * 128
        per = min(per, 512)
        col_groups = []
        c0 = 0
        while c0 < ncols_total:
            w = min(per, ncols_total - c0)
            col_groups.append([(c0, w)])
            c0 += w
    for segs in col_groups:
        width = sum(w for _, w in segs)

        def mk_loads(segs=segs, width=width):
            loads = []
            off = 0
            for (c0, w) in segs:
                def lf(slab, c0=c0, w=w, off=off):
                    dst = slab[:, 0:kchunks * width].rearrange("p (k n) -> p k n", k=kchunks)[:, :, off:off + w]
                    return dst, wv[:, :, c0:c0 + w], wkey
                loads.append(lf)
                off += w
            return loads

        def compute(slab, skey, segs=segs, width=width):
            sv = slab[:, 0:kchunks * width].rearrange("p (k n) -> p k n", k=kchunks)
            off = 0
            for (c0, w) in segs:
                for o in range(w // 128):
                    ps, pk = b.bank()
                    for kc in range(kchunks):
                        S.op('pe', (lambda kc, o, off, ps: lambda e: e.matmul(ps[:, 0:n], lhsT=sv[:, kc, off + o * 128:off + (o + 1) * 128],
                                                                                rhs=rhs_fn(kc), start=(kc == 0), stop=(kc == kchunks - 1)))(kc, o, off, ps),
                             reads=[skey, rhs_keys(kc)], writes=[pk])
                    handler((c0 + o * 128) // 128, ps, pk)
                off += w
        b.job(mk_loads(), compute)


def emit_linear_perm(b, wsrc, wkey, ncols_total, kchunks, rhs_fn, rhs_keys, handler, n, d):
    S = b.S
    wv = wsrc.rearrange("(k p) n -> p k n", p=128)
    width = ncols_total

    def lf(slab):
        dst = slab[:, 0:kchunks * width].rearrange("p (k n) -> p k n", k=kchunks)
        return dst, wv, wkey

    def compute(slab, skey):
        sv = slab[:, 0:kchunks * width].rearrange("p (k n) -> p k n", k=kchunks)
        for o in range(width // 128):
            ps, pk = b.bank()
            out = ps[:, 0:n] if d == 1 else ps[:, 0:n].rearrange("p (ph l) -> p ph l", ph=d)
            for kc in range(kchunks):
                S.op('pe', (lambda kc, o, out: lambda e: e.matmul(out, lhsT=sv[:, kc, o * 128:(o + 1) * 128], rhs=rhs_fn(kc),
                                                                   start=(kc == 0), stop=(kc == kchunks - 1)))(kc, o, out),
                     reads=[skey, rhs_keys(kc)], writes=[pk])
            handler(o, ps, pk)
    b.job([lf], compute)


def emit_ffn(b, xt, xkey, gcol, pv, hn, ones, sq, rt, w_in, w_in_key, w_out, w_out_key, sg, mt):
    S = b.S
    emit_rmsnorm(b, xt, xkey, T, gcol, pv, hn, 'hn', ones, sq, rt, 'ffn')
    groups = []
    for c in range(0, FC, 2):
        groups.append([(c * 128, 256), (FFN + c * 128, 256)])

    def h_in(oc, ps, pk):
        if oc < FC:
            S.op('act', lambda e: e.activation(out=sg[:, oc % 2, :], in_=ps[:], func=AF.Silu), reads=[pk], writes=[('sg', oc % 2)])
        else:
            c = oc - FC
            S.op('dve', lambda e: e.tensor_tensor(out=mt[:, c, :], in0=ps[:], in1=sg[:, c % 2, :], op=ALU.mult),
                 reads=[pk, ('sg', c % 2)], writes=[('mt', c)])
    emit_linear(b, w_in, w_in_key, 2 * FFN, KC, lambda kc: hn[:, kc, :], lambda kc: ('hn', kc), h_in, T, col_groups=groups)

    def h_out(oc, ps, pk):
        S.op('dve', lambda e: e.tensor_tensor(out=xt[:, oc, 0:T], in0=ps[:], in1=xt[:, oc, 0:T], op=ALU.add),
             reads=[pk, (xkey, oc)], writes=[(xkey, oc)])
    emit_linear(b, w_out, w_out_key, D, FC, lambda kc: mt[:, kc, :], lambda kc: ('mt', kc), h_out, T,
                col_groups=[[(c * 128, 128)] for c in range(KC)])


def emit_rope_tables(b, pos32, pv, i, tb):
    S = b.S
    pi, pf, an, kf, ki, m, C, Sn, tmp = tb['pi'], tb['pf'], tb['an'], tb['kf'], tb['ki'], tb['m'], tb['C'], tb['S'], tb['tmp']

    def compute(_s, _k):
        S.dma('sp', 'pos', lambda e: e.dma_start(out=pi[:], in_=pos32[:, i * T:(i + 1) * T]), writes=['pi'])
        S.op('pool', lambda e: e.tensor_copy(out=pf[:], in_=pi[:]), reads=['pi'], writes=['pf'])
        S.op('pool', lambda e: e.tensor_scalar(out=an[:], in0=pf[:], scalar1=pv[0:32, PV_INVF:PV_INVF + 1], scalar2=None, op0=ALU.mult),
             reads=['pf', 'pv'], writes=['an'])
        S.op('pool', lambda e: e.tensor_scalar(out=kf[:], in0=an[:], scalar1=1.0 / TWO_PI, scalar2=None, op0=ALU.mult), reads=['an'], writes=['kf'])
        S.op('pool', lambda e: e.tensor_copy(out=ki[:], in_=kf[:]), reads=['kf'], writes=['ki'])
        S.op('pool', lambda e: e.tensor_copy(out=kf[:], in_=ki[:]), reads=['ki'], writes=['kf'])
        S.op('pool', lambda e: e.tensor_scalar(out=tmp[:], in0=kf[:], scalar1=-CW1, scalar2=None, op0=ALU.mult), reads=['kf'], writes=['tmp'])
        S.op('pool', lambda e: e.tensor_tensor(out=an[:], in0=an[:], in1=tmp[:], op=ALU.add), reads=['tmp', 'an'], writes=['an'])
        S.op('pool', lambda e: e.tensor_scalar(out=tmp[:], in0=kf[:], scalar1=-CW2, scalar2=None, op0=ALU.mult), reads=['kf'], writes=['tmp'])
        S.op('pool', lambda e: e.tensor_tensor(out=an[:], in0=an[:], in1=tmp[:], op=ALU.add), reads=['tmp', 'an'], writes=['an'])
        S.op('pool', lambda e: e.tensor_scalar(out=m[:], in0=an[:], scalar1=math.pi, scalar2=None, op0=ALU.is_gt), reads=['an'], writes=['m'])
        S.op('pool', lambda e: e.tensor_scalar(out=tmp[:], in0=m[:], scalar1=-TWO_PI, scalar2=None, op0=ALU.mult), reads=['m'], writes=['tmp'])
        S.op('pool', lambda e: e.tensor_tensor(out=an[:], in0=an[:], in1=tmp[:], op=ALU.add), reads=['tmp', 'an'], writes=['an'])
        S.op('pool', lambda e: e.tensor_scalar(out=m[:], in0=an[:], scalar1=-math.pi, scalar2=None, op0=ALU.is_lt), reads=['an'], writes=['m'])
        S.op('pool', lambda e: e.tensor_scalar(out=tmp[:], in0=m[:], scalar1=TWO_PI, scalar2=None, op0=ALU.mult), reads=['m'], writes=['tmp'])
        S.op('pool', lambda e: e.tensor_tensor(out=an[:], in0=an[:], in1=tmp[:], op=ALU.add), reads=['tmp', 'an'], writes=['an'])
        S.op('act', lambda e: e.activation(out=Sn[:], in_=an[:], func=AF.Sin), reads=['an'], writes=['S'])
        S.op('pool', lambda e: e.tensor_scalar(out=pf[:], in0=an[:], scalar1=math.pi / 2, scalar2=None, op0=ALU.add), reads=['an'], writes=['pf'])
        S.op('pool', lambda e: e.tensor_scalar(out=m[:], in0=pf[:], scalar1=math.pi, scalar2=None, op0=ALU.is_gt), reads=['pf'], writes=['m'])
        S.op('pool', lambda e: e.tensor_scalar(out=tmp[:], in0=m[:], scalar1=-TWO_PI, scalar2=None, op0=ALU.mult), reads=['m'], writes=['tmp'])
        S.op('pool', lambda e: e.tensor_tensor(out=pf[:], in0=pf[:], in1=tmp[:], op=ALU.add), reads=['tmp', 'pf'], writes=['pf'])
        S.op('act', lambda e: e.activation(out=C[:], in_=pf[:], func=AF.Sin), reads=['pf'], writes=['C'])
    b.job([], compute)


def emit_rope_tables_all(b, pos32, pv, dd_):
    S = b.S
    PW = 1024
    pi = b.sb("tq_pi", [32, PW], I32)
    pf = b.sb("tq_pf", [32, PW], F32)
    an = b.sb("tq_an", [32, PW], F32)
    kf = b.sb("tq_kf", [32, PW], F32)
    m = b.sb("tq_m", [32, PW], F32)
    C = b.sb("tq_C", [32, PW], F32)
    Sn = b.sb("tq_S", [32, PW], F32)
    for pc in range(S_CORE // PW):
        sl = slice(pc * PW, (pc + 1) * PW)
        S.dma('sp', 'pos', (lambda sl: lambda e: e.dma_start(out=pi[:], in_=pos32[:, sl]))(sl), writes=['tq_pi'])
        S.op('dve', lambda e: e.tensor_copy(out=pf[:], in_=pi[:]), reads=['tq_pi'], writes=['tq_pf'])
        S.op('dve', lambda e: e.tensor_scalar(out=an[:], in0=pf[:], scalar1=pv[0:32, PV_INVF:PV_INVF + 1], scalar2=None, op0=ALU.mult), reads=['tq_pf', 'pv'], writes=['tq_an'])
        S.op('dve', lambda e: e.tensor_scalar(out=kf[:], in0=an[:], scalar1=1.0 / TWO_PI, scalar2=None, op0=ALU.mult), reads=['tq_an'], writes=['tq_kf'])
        S.op('dve', lambda e: e.tensor_copy(out=pi[:], in_=kf[:]), reads=['tq_kf', 'tq_pf'], writes=['tq_pi'])
        S.op('dve', lambda e: e.tensor_copy(out=kf[:], in_=pi[:]), reads=['tq_pi'], writes=['tq_kf'])
        S.op('dve', lambda e: e.scalar_tensor_tensor(out=an[:], in0=kf[:], scalar=-CW1, in1=an[:], op0=ALU.mult, op1=ALU.add), reads=['tq_kf', 'tq_an'], writes=['tq_an'])
        S.op('dve', lambda e: e.scalar_tensor_tensor(out=an[:], in0=kf[:], scalar=-CW2, in1=an[:], op0=ALU.mult, op1=ALU.add), reads=['tq_kf', 'tq_an'], writes=['tq_an'])
        S.op('dve', lambda e: e.tensor_scalar(out=m[:], in0=an[:], scalar1=math.pi, scalar2=None, op0=ALU.is_gt), reads=['tq_an'], writes=['tq_m'])
        S.op('dve', lambda e: e.scalar_tensor_tensor(out=an[:], in0=m[:], scalar=-TWO_PI, in1=an[:], op0=ALU.mult, op1=ALU.add), reads=['tq_m', 'tq_an'], writes=['tq_an'])
        S.op('dve', lambda e: e.tensor_scalar(out=m[:], in0=an[:], scalar1=-math.pi, scalar2=None, op0=ALU.is_lt), reads=['tq_an'], writes=['tq_m'])
        S.op('dve', lambda e: e.scalar_tensor_tensor(out=an[:], in0=m[:], scalar=TWO_PI, in1=an[:], op0=ALU.mult, op1=ALU.add), reads=['tq_m', 'tq_an'], writes=['tq_an'])
        S.op('act', lambda e: e.activation(out=Sn[:], in_=an[:], func=AF.Sin), reads=['tq_an'], writes=['tq_S'])
        S.op('dve', lambda e: e.tensor_scalar(out=pf[:], in0=an[:], scalar1=math.pi / 2, scalar2=None, op0=ALU.add), reads=['tq_an'], writes=['tq_pf'])
        S.op('dve', lambda e: e.tensor_scalar(out=m[:], in0=pf[:], scalar1=math.pi, scalar2=None, op0=ALU.is_gt), reads=['tq_pf'], writes=['tq_m'])
        S.op('dve', lambda e: e.scalar_tensor_tensor(out=pf[:], in0=m[:], scalar=-TWO_PI, in1=pf[:], op0=ALU.mult, op1=ALU.add), reads=['tq_m', 'tq_pf'], writes=['tq_pf'])
        S.op('act', lambda e: e.activation(out=C[:], in_=pf[:], func=AF.Sin), reads=['tq_pf'], writes=['tq_C'])
        S.dma('sp', 'cto', (lambda sl: lambda e: e.dma_start(out=dd_['ctab'][:, sl], in_=C[:]))(sl), reads=['tq_C'])
        S.dma('sp', 'sto', (lambda sl: lambda e: e.dma_start(out=dd_['stab'][:, sl], in_=Sn[:]))(sl), reads=['tq_S'])


def perm_view(ap2d, d):
    if d == 1:
        return ap2d
    return ap2d.rearrange("p (l ph) -> p ph l", ph=d)


def emit_head_post(b, ps, pk, gcol_ap, permG, pgkey, tb, d, dst, dkey, ones, hw):
    S = b.S
    n = hw['i'][0] % len(hw['sets'])
    hw['i'][0] += 1
    st = hw['sets'][n]
    sqh, x32, rth, t1, t2 = st['sq'], st['x32'], st['rt'], st['t1'], st['t2']
    ksq, kx, krt, k1, k2 = ('hsq', n), ('hx32', n), ('hrt', n), ('ht1', n), ('ht2', n)
    Cv = perm_view(tb['C'][:], d)
    Sv = perm_view(tb['S'][:], d)
    pv3 = (lambda a: a) if d == 1 else (lambda a: a.rearrange("p (ph l) -> p ph l", ph=d))
    S.op('act', lambda e: e.activation(out=sqh[:], in_=ps[:], func=AF.Square), reads=[pk], writes=[ksq])
    S.op('act', lambda e: e.activation(out=x32[:], in_=ps[0:32, :], func=AF.Identity), reads=[pk], writes=[kx])
    ps2, pk2 = b.bank()
    S.op('pe', lambda e: e.matmul(ps2[:], lhsT=ones[:], rhs=sqh[:], start=True, stop=True), reads=[ksq, 'ones'], writes=[pk2])
    ps3, pk3 = b.bank()
    S.op('pe', lambda e: e.matmul(ps3[0:32, :], lhsT=permG, rhs=x32[:], start=True, stop=True), reads=[kx, pgkey], writes=[pk3])
    S.op('act', lambda e: e.activation(out=rth[:], in_=ps2[:], func=AF.Sqrt, scale=1.0 / 128, bias=hw['eps']), reads=[pk2, 'pv'], writes=[krt])
    S.op('dve', lambda e: e.reciprocal(out=rth[:], in_=rth[:]), reads=[krt], writes=[krt])
    S.op('dve', lambda e: e.scalar_tensor_tensor(out=dst, in0=ps[:], scalar=gcol_ap, in1=rth[:], op0=ALU.mult, op1=ALU.mult),
         reads=[pk, krt, 'pv'], writes=[dkey])
    S.op('dve', lambda e: e.scalar_tensor_tensor(out=pv3(t1[:]), in0=pv3(ps[0:32, :]), scalar=gcol_ap[0:32, :], in1=Cv, op0=ALU.mult, op1=ALU.mult),
         reads=[pk, tb.get('Ckey', 'C'), 'pv'], writes=[k1])
    S.op('dve', lambda e: e.tensor_tensor(out=pv3(t2[:]), in0=pv3(ps3[0:32, :]), in1=Sv, op=ALU.mult), reads=[pk3, tb.get('Skey', 'S')], writes=[k2])
    S.op('pool', lambda e: e.tensor_tensor(out=t1[:], in0=t1[:], in1=t2[:], op=ALU.add), reads=[k1, k2], writes=[k1])
    S.op('pool', lambda e: e.tensor_tensor(out=dst[0:32, :], in0=t1[:], in1=rth[0:32, :], op=ALU.mult), reads=[k1, krt], writes=[dkey])


def make_hw(b, pv, nsets):
    sets = []
    for n in range(nsets):
        sets.append({'sq': b.sb("hsq%d" % n, [128, T], BF16), 'x32': b.sb("hx32_%d" % n, [32, T], BF16), 'rt': b.sb("hrt%d" % n, [128, T], F32),
                     't1': b.sb("ht1_%d" % n, [32, T], F32), 't2': b.sb("ht2_%d" % n, [32, T], F32)})
    return {'sets': sets, 'i': [0], 'eps': pv[:, PV_EPS:PV_EPS + 1]}


def make_permG(b, p0, pv, gcol, name):
    t = b.sb(name, [32, 32], BF16)
    b.S.op('dve', lambda e: e.tensor_scalar(out=t[:], in0=p0[:], scalar1=pv[0:32, gcol:gcol + 1], scalar2=None, op0=ALU.mult),
           reads=['p0', 'pv'], writes=[name])
    return t


def cast_weight(b, name, src_ap, shape, extra_reads=()):
    if name in b.wcache:
        return b.wcache[name]
    dst = b.dram_tmp("wb_" + name, shape, BF16)
    rows = shape[0]
    step = max(128, rows // 4)
    r0 = 0
    keys = []
    while r0 < rows:
        r1 = min(rows, r0 + step)
        ck = b.wcache.get('_ci', 0)
        b.wcache['_ci'] = ck + 1
        b.S.dma('pool', 'k%d' % (ck % 4), (lambda r0, r1: lambda e: e.dma_start(out=dst[r0:r1, :], in_=src_ap[r0:r1, :]))(r0, r1),
                reads=list(extra_reads), writes=[('wb', name, r0)])
        keys.append(('wb', name, r0))
        r0 = r1
    b.wcache[name] = (dst, keys)
    return dst, keys


NB = 4


def load_tables(b, dd_, i, Ct, St):
    S = b.S

    def compute(_s, _k):
        S.dma('sp', 'ct%d' % (i % 2), lambda e: e.dma_start(out=Ct[i % 2][:], in_=dd_['ctab'][:, i * T:(i + 1) * T]), writes=[('Ct', i % 2)])
        S.dma('sp', 'st%d' % (i % 2), lambda e: e.dma_start(out=St[i % 2][:], in_=dd_['stab'][:, i * T:(i + 1) * T]), writes=[('St', i % 2)])
    b.job([], compute)
    return {'C': Ct[i % 2], 'S': St[i % 2], 'Ckey': ('Ct', i % 2), 'Skey': ('St', i % 2)}


def build_stageA(mode, prog):
    nc = prog['nc']
    dd_ = prog['dram']
    es = ExitStack()
    with es:
        b = Bld(nc, es, pfx="a%d_" % mode, sem_es=prog['sem_es'], dram=prog['dram'], wcache=prog['wcache'], S=prog['S'])
        S = b.S
        xT = dd_["xT"]
        xh = dd_["xh"]
        pvd = dd_["pvec"]
        pos32 = dd_["pos32"]
        asp, usp = dd_["asp"], dd_["usp"]
        h1T = dd_["h1T"]
        KTo = [dd_["KT%d" % g] for g in range(3)]
        Vo = [dd_["V%d" % g] for g in range(3)]

        pv = b.sb("pv", [128, NPVT], F32)
        b.cload(pv[:], pvd, 'pv')
        ones = b.sb("ones", [128, 128], BF16)
        S.op('pool', lambda e: e.memset(ones[:], 1.0), writes=['ones'])
        wb_in, k_in = cast_weight(b, "a_w_in", dd_["a_w_in"], [D, 2 * D])
        xt = [b.sb("xt%d" % i, [128, KC, T], F32) for i in range(2)]
        hn = b.sb("hn", [128, KC, T], BF16)
        sq = b.sb("sq", [128, 2, T], BF16)
        rt = b.sb("rt", [128, T], F32)
        state = b.sb("state", [128, 8], F32)

        def xload(i):
            def ld(_s, _k, i=i):
                x = xt[i % 2]
                xkey = 'x%d' % (i % 2)
                S.dma('sp', xkey, lambda e: e.dma_start(out=x[:], in_=xT[:, i * T:(i + 1) * T].rearrange("(c p) n -> p c n", p=128)),
                      writes=[(xkey, c) for c in range(KC)])
            b.job([], ld)

        if mode == 1:
            emit_rope_tables_all(b, pos32, pv, dd_)
            ident = b.sb("ident", [128, 128], BF16)
            b.cload(ident[:], dd_["ident"], 'ident')
            gaw_s = b.sb("gaw_s", [128, 8, 128], BF16)
            gxw_s = b.sb("gxw_s", [128, 8, 128], BF16)
            S.dma('pool', 'gw', lambda e: e.dma_start(out=gaw_s[:], in_=dd_["gaw"]), writes=['gaw'])
            S.dma('pool', 'gw', lambda e: e.dma_start(out=gxw_s[:], in_=dd_["gxw"]), writes=['gxw'])
            dg = b.sb("dg", [128, 32, 128], F32)
            for kc_ in range(32):
                S.op('dve', (lambda kc_: lambda e: e.tensor_scalar(out=dg[:, kc_, :], in0=ident[:], scalar1=pv[:, PV_CONVW + kc_:PV_CONVW + kc_ + 1], scalar2=None, op0=ALU.mult))(kc_),
                     reads=['ident', 'pv'], writes=[('dg', kc_)])
            cl = b.sb("cl", [128, 8], F32)
            yy = b.sb("yy", [128, 8], F32)
            ser = b.sb("ser", [128, 8], F32)
            lnv = b.sb("lnv", [128, 8], F32)
            msk = b.sb("msk", [128, 8], F32)
            lam = pv[:, PV_LAM:PV_LAM + 8]
            S.op('act', lambda e: e.activation(out=yy[:], in_=lam, func=AF.Exp, scale=-1.0), reads=['pv'], writes=['yy'])
            S.op('dve', lambda e: e.tensor_scalar(out=ser[:], in0=yy[:], scalar1=-0.25, scalar2=1.0 / 3, op0=ALU.mult, op1=ALU.add), reads=['yy'], writes=['ser'])
            S.op('dve', lambda e: e.tensor_tensor(out=ser[:], in0=ser[:], in1=yy[:], op=ALU.mult), reads=['ser', 'yy'], writes=['ser'])
            S.op('dve', lambda e: e.tensor_scalar(out=ser[:], in0=ser[:], scalar1=-1.0, scalar2=0.5, op0=ALU.mult, op1=ALU.add), reads=['ser'], writes=['ser'])
            S.op('dve', lambda e: e.tensor_tensor(out=ser[:], in0=ser[:], in1=yy[:], op=ALU.mult), reads=['ser', 'yy'], writes=['ser'])
            S.op('dve', lambda e: e.tensor_scalar(out=ser[:], in0=ser[:], scalar1=-1.0, scalar2=1.0, op0=ALU.mult, op1=ALU.add), reads=['ser'], writes=['ser'])
            S.op('dve', lambda e: e.tensor_tensor(out=ser[:], in0=ser[:], in1=yy[:], op=ALU.mult), reads=['ser', 'yy'], writes=['ser'])
            S.op('act', lambda e: e.activation(out=lnv[:], in_=yy[:], func=AF.Ln, scale=1.0, bias=pv[:, PV_ONE:PV_ONE + 1]), reads=['yy', 'pv'], writes=['lnv'])
            S.op('dve', lambda e: e.tensor_scalar(out=msk[:], in0=yy[:], scalar1=0.05, scalar2=None, op0=ALU.is_lt), reads=['yy'], writes=['msk'])
            S.op('dve', lambda e: e.tensor_tensor(out=ser[:], in0=ser[:], in1=lnv[:], op=ALU.subtract), reads=['ser', 'lnv'], writes=['ser'])
            S.op('dve', lambda e: e.tensor_tensor(out=ser[:], in0=ser[:], in1=msk[:], op=ALU.mult), reads=['ser', 'msk'], writes=['ser'])
            S.op('dve', lambda e: e.tensor_tensor(out=ser[:], in0=ser[:], in1=lnv[:], op=ALU.add), reads=['ser', 'lnv'], writes=['ser'])
            S.op('dve', lambda e: e.tensor_scalar(out=cl[:], in0=ser[:], scalar1=-8.0, scalar2=None, op0=ALU.mult), reads=['ser'], writes=['cl'])

            xb = b.sb("xb", [128, KC, T + 3], F32)
            xhs = b.sb("xhs", [128, KC, 4], F32)
            hnh = b.sb("hnh", [128, KC, 4], BF16)
            xc = b.sb("xc", [128, 2, T], F32)
            xcb = b.sb("xcb", [128, 2, T], BF16)
            rg = b.sb("rg", [128, 2, T], F32)
            ig = b.sb("ig", [128, 2, T], F32)
            av = b.sb("av", [128, NB, T], F32)
            a2 = b.sb("a2", [128, 2, T], F32)
            uv = b.sb("uv", [128, NB, T], F32)
            hr = b.sb("hr", [128, 2, T], F32)
            rsum = b.sb("rsum", [128, 8, NT], F32)
            S.op('dve', lambda e: e.memset(state[:], 0.0), writes=[('state', c) for c in range(KC)])

            b.cload(xhs[:], xh.rearrange("(c p) n -> p c n", p=128), [('xh', c) for c in range(KC)])
            emit_rmsnorm(b, xhs, 'xh', 4, PV_ANORM, pv, hnh, 'hnh', ones, sq, rt, 'halo')

            def h_halo(oc, ps, pk):
                S.op('act', lambda e: e.activation(out=xb[:, oc, T:T + 3], in_=ps[:, 1:4], func=AF.Identity), reads=[pk], writes=[('xb', oc)])
            emit_linear(b, wb_in[:, D:2 * D], k_in, D, KC, lambda kc: hnh[:, kc, :], lambda kc: ('hnh', kc), h_halo, 4)

            xload(0)
            for i in range(NT):
                x = xt[i % 2]
                xkey = 'x%d' % (i % 2)
                if i + 1 < NT:
                    xload(i + 1)
                emit_rmsnorm(b, x, xkey, T, PV_ANORM, pv, hn, 'hn', ones, sq, rt, 'a')

                def h_xb(oc, ps, pk):
                    S.op('dve', lambda e: e.tensor_copy(out=xb[:, oc, 0:3], in_=xb[:, oc, T:T + 3]), reads=[('xb', oc)], writes=[('xb', oc)])
                    S.op('dve', lambda e: e.tensor_copy(out=xb[:, oc, 3:T + 3], in_=ps[:]), reads=[pk], writes=[('xb', oc)])
                emit_linear(b, wb_in[:, D:2 * D], k_in, D, KC, lambda kc: hn[:, kc, :], lambda kc: ('hn', kc), h_xb, T)

                def lru(_s, _k, i=i):
                    def steps_for(c):
                        j = c % NB
                        q = c % 2
                        st_ = {}
                        sl = []

                        def s_conv():
                            st_['psc'], st_['pkc'] = b.bank()
                            psc, pkc = st_['psc'], st_['pkc']
                            for k in range(4):
                                S.op('pe', (lambda k: lambda e: e.matmul(psc[:], lhsT=dg[:, 8 * k + c, :], rhs=xb[:, c, k:k + T], start=(k == 0), stop=(k == 3)))(k),
                                     reads=[('dg', 8 * k + c), ('xb', c)], writes=[pkc])
                        sl.append(s_conv)

                        def s_xc():
                            psc, pkc = st_['psc'], st_['pkc']
                            S.op('act', lambda e: e.activation(out=xc[:, q, :], in_=psc[:], func=AF.Identity, bias=pv[:, PV_CONVB + c:PV_CONVB + c + 1]),
                                 reads=[pkc, 'pv'], writes=[('xc', q)])
                            S.op('dve', lambda e: e.tensor_scalar(out=xcb[:, q, :], in0=psc[:], scalar1=pv[:, PV_CONVB + c:PV_CONVB + c + 1], scalar2=None, op0=ALU.add),
                                 reads=[pkc, 'pv'], writes=[('xcb', q)])
                        sl.append(s_xc)

                        def s_gates():
                            st_['ps1'], st_['pk1'] = b.bank()
                            st_['ps2'], st_['pk2'] = b.bank()
                            ps1, ps2 = st_['ps1'], st_['ps2']
                            S.op('pe', lambda e: e.matmul(ps1[:], lhsT=gaw_s[:, c, :], rhs=xcb[:, q, :], start=True, stop=True), reads=['gaw', ('xcb', q)], writes=[st_['pk1']])
                            S.op('pe', lambda e: e.matmul(ps2[:], lhsT=gxw_s[:, c, :], rhs=xcb[:, q, :], start=True, stop=True), reads=['gxw', ('xcb', q)], writes=[st_['pk2']])
                        sl.append(s_gates)

                        def s_sig():
                            ps1, ps2 = st_['ps1'], st_['ps2']
                            S.op('act', lambda e: e.activation(out=rg[:, q, :], in_=ps1[:], func=AF.Sigmoid, bias=pv[:, PV_GAB + c:PV_GAB + c + 1], accum_out=rsum[:, c, i:i + 1]),
                                 reads=[st_['pk1'], 'pv'], writes=[('rg', q), ('rsum', c, i)])
                            S.op('act', lambda e: e.activation(out=ig[:, q, :], in_=ps2[:], func=AF.Sigmoid, bias=pv[:, PV_GXB + c:PV_GXB + c + 1]),
                                 reads=[st_['pk2'], 'pv'], writes=[('ig', q)])
                        sl.append(s_sig)

                        def s_exp():
                            S.op('act', lambda e: e.activation(out=av[:, j, :], in_=rg[:, q, :], func=AF.Exp, scale=cl[:, c:c + 1]), reads=[('rg', q), 'cl'], writes=[('av', j)])
                            S.op('dve', lambda e: e.tensor_tensor(out=uv[:, j, :], in0=ig[:, q, :], in1=xc[:, q, :], op=ALU.mult), reads=[('ig', q), ('xc', q)], writes=[('uv', j)])
                        sl.append(s_exp)

                        def s_sq():
                            S.op('dve', lambda e: e.tensor_tensor(out=a2[:, q, :], in0=av[:, j, :], in1=av[:, j, :], op=ALU.mult), reads=[('av', j)], writes=[('a2', q)])
                        sl.append(s_sq)

                        def s_sqrt():
                            S.op('act', lambda e: e.activation(out=a2[:, q, :], in_=a2[:, q, :], func=AF.Sqrt, scale=-1.0, bias=pv[:, PV_ONE:PV_ONE + 1]), reads=[('a2', q), 'pv'], writes=[('a2', q)])
                        sl.append(s_sqrt)

                        def s_u():
                            S.op('dve', lambda e: e.tensor_tensor(out=uv[:, j, :], in0=uv[:, j, :], in1=a2[:, q, :], op=ALU.mult), reads=[('uv', j), ('a2', q)], writes=[('uv', j)])
                            S.dma('sp', 'sa%d' % j, lambda e: e.dma_start(out=asp[:, i, c, :], in_=av[:, j, :]), reads=[('av', j)])
                            S.dma('sp', 'su%d' % j, lambda e: e.dma_start(out=usp[:, i, c, :], in_=uv[:, j, :]), reads=[('uv', j)])
                        sl.append(s_u)

                        def s_scan():
                            S.op('dve', lambda e: e.tensor_tensor_scan(out=hr[:, c % 2, :], data0=av[:, j, :], data1=uv[:, j, :], initial=state[:, c:c + 1], op0=ALU.mult, op1=ALU.add),
                                 reads=[('av', j), ('uv', j), ('state', c)], writes=[('hr', c % 2)])
                        sl.append(s_scan)

                        def s_state():
                            S.op('dve', lambda e: e.tensor_copy(out=state[:, c:c + 1], in_=hr[:, c % 2, T - 1:T]), reads=[('hr', c % 2)], writes=[('state', c)])
                        sl.append(s_state)
                        return sl

                    for c0 in range(0, KC, 2):
                        sa_, sb_ = steps_for(c0), steps_for(c0 + 1)
                        for fa, fb in zip(sa_, sb_):
                            fa()
                            fb()
                b.job([], lru)
                if i == 0:
                    def latecast(_s, _k):
                        er = [('av', 0)]
                        cast_weight(b, "a_w_out", dd_["a_w_out"], [D, D], er)
                        cast_weight(b, "a_ffn_w_in", dd_["a_ffn_w_in"], [D, 2 * FFN], er)
                        cast_weight(b, "a_ffn_w_out", dd_["a_ffn_w_out"], [FFN, D], er)
                        cast_weight(b, "w_kv", dd_["w_kv"], [D, 1536], er)
                    b.job([], latecast)

            def fin(_s, _k):
                rs = b.sb("rs", [128, 8], F32)
                ab = b.sb("ab", [128, 16], F32)
                S.op('dve', lambda e: e.reduce_sum(out=rs[:], in_=rsum[:], axis=AX.X), reads=[('rsum', c, i) for c in range(8) for i in range(NT)], writes=['rs'])
                S.op('dve', lambda e: e.tensor_tensor(out=rs[:], in0=rs[:], in1=cl[:], op=ALU.mult), reads=['rs', 'cl'], writes=['rs'])
                S.op('act', lambda e: e.activation(out=ab[:, 0:8], in_=rs[:], func=AF.Exp), reads=['rs'], writes=['ab'])
                S.op('dve', lambda e: e.tensor_copy(out=ab[:, 8:16], in_=state[:]), reads=[('state', c) for c in range(KC)] + ['ab'], writes=['ab'])
                S.dma('sp', 'abo', lambda e: e.dma_start(out=dd_['AB'], in_=ab[:]), reads=['ab'])
            b.job([], fin)
            b.run_jobs()
            cast_weight(b, "b_w_q", dd_["b_w_q"], [D, 3072])
            cast_weight(b, "b_w_o", dd_["b_w_o"], [D, D])
            cast_weight(b, "b_ffn_w_in", dd_["b_ffn_w_in"], [D, 2 * FFN])
            cast_weight(b, "b_ffn_w_out", dd_["b_ffn_w_out"], [FFN, D])
            S.barrier_all(skip=('k0', 'k1', 'k2', 'k3'))
            S.dma('pool', 'cc', lambda e: e.collective_compute("AllGather", op=ALU.bypass, replica_groups=[[0, 1, 2, 3], [4, 5, 6, 7]],
                                                               ins=[dd_['AB'].opt()], outs=[dd_['ABg'].opt()]), inc=1)
            S.barrier_all(skip=('k0', 'k1', 'k2', 'k3'))
            return

        wb_out, k_out = cast_weight(b, "a_w_out", None, None)
        wb_fin, k_fin = cast_weight(b, "a_ffn_w_in", None, None)
        wb_fout, k_fout = cast_weight(b, "a_ffn_w_out", None, None)
        wb_kv, k_kv = cast_weight(b, "w_kv", None, None)
        av8 = b.sb("av8", [128, KC, T], F32)
        uv8 = b.sb("uv8", [128, KC, T], F32)
        hr = b.sb("hr", [128, 2, T], F32)
        gate = b.sb("gate", [128, KC, T], BF16)
        gh = b.sb("gh", [128, KC, T], BF16)
        sg = b.sb("sg", [128, 2, T], BF16)
        mt = b.sb("mt", [128, FC, T], BF16)
        aball = b.sb("aball", [128, 4, 16], F32)
        ae = b.sb("ae", [128, 8], F32)
        be = b.sb("be", [128, 8], F32)
        p0 = b.sb("p0s", [32, 32], F32)
        b.cload(p0[:], dd_["p0"], 'p0')
        b.cload(aball[:], dd_["ABall"], 'aball')
        Ct = [b.sb("Ct%d" % i, [32, T], F32) for i in range(2)]
        St = [b.sb("St%d" % i, [32, T], F32) for i in range(2)]
        hw = make_hw(b, pv, 2)
        permK = [make_permG(b, p0, pv, PV_KN + g, "permK%d" % g) for g in range(3)]
        kts = b.sb("kts", [128, 2, T], BF16)
        vts = b.sb("vts", [128, 2, 256], BF16)
        vbi = [0]
        voi = [0]

        S.op('dve', lambda e: e.memset(state[:], 0.0), writes=['state'])
        for i in range(4):
            A_i = aball[:, i, 0:8]
            B_i = aball[:, i, 8:16]
            sel = pv[:, PV_SEL + i:PV_SEL + i + 1]
            S.op('dve', (lambda A_i, sel: lambda e: e.tensor_scalar(out=ae[:], in0=A_i, scalar1=-1.0, scalar2=sel, op0=ALU.add, op1=ALU.mult))(A_i, sel),
                 reads=['aball', 'pv'], writes=['ae'])
            S.op('dve', lambda e: e.tensor_scalar(out=ae[:], in0=ae[:], scalar1=1.0, scalar2=None, op0=ALU.add), reads=['ae'], writes=['ae'])
            S.op('dve', (lambda B_i, sel: lambda e: e.tensor_scalar(out=be[:], in0=B_i, scalar1=sel, scalar2=None, op0=ALU.mult))(B_i, sel),
                 reads=['aball', 'pv'], writes=['be'])
            S.op('dve', lambda e: e.tensor_tensor(out=state[:], in0=state[:], in1=ae[:], op=ALU.mult), reads=['state', 'ae'], writes=['state'])
            S.op('dve', lambda e: e.tensor_tensor(out=state[:], in0=state[:], in1=be[:], op=ALU.add), reads=['state', 'be'], writes=['state'])

        xload(0)
        for i in range(NT):
            x = xt[i % 2]
            xkey = 'x%d' % (i % 2)
            if i + 1 < NT:
                xload(i + 1)

            def aul(_s, _k, i=i):
                S.dma('sp', 'la', lambda e: e.dma_start(out=av8[:], in_=asp[:, i, :, :]), writes=[('av8', c) for c in range(KC)])
                S.dma('sp', 'lu', lambda e: e.dma_start(out=uv8[:], in_=usp[:, i, :, :]), writes=[('uv8', c) for c in range(KC)])
            b.job([], aul)
            tbl = load_tables(b, dd_, i, Ct, St)
            emit_rmsnorm(b, x, xkey, T, PV_ANORM, pv, hn, 'hn', ones, sq, rt, 'a')

            def h_gate(oc, ps, pk):
                S.op('act', lambda e: e.activation(out=gate[:, oc, :], in_=ps[:], func=AF.Gelu_apprx_tanh), reads=[pk], writes=[('gate', oc)])
            emit_linear(b, wb_in[:, 0:D], k_in, D, KC, lambda kc: hn[:, kc, :], lambda kc: ('hn', kc), h_gate, T)

            def lru2(_s, _k, i=i):
                for c in range(KC):
                    j = c % 2
                    S.op('dve', (lambda c, j: lambda e: e.tensor_tensor_scan(out=hr[:, j, :], data0=av8[:, c, :], data1=uv8[:, c, :], initial=state[:, c:c + 1],
                                                                              op0=ALU.mult, op1=ALU.add))(c, j),
                         reads=[('av8', c), ('uv8', c), 'state'], writes=[('hr', j)])
                    S.op('dve', (lambda c, j: lambda e: e.tensor_copy(out=state[:, c:c + 1], in_=hr[:, j, T - 1:T]))(c, j), reads=[('hr', j)], writes=['state'])
                    S.op('pool', (lambda c, j: lambda e: e.tensor_tensor(out=gh[:, c, :], in0=gate[:, c, :], in1=hr[:, j, :], op=ALU.mult))(c, j),
                         reads=[('gate', c), ('hr', j)], writes=[('gh', c)])
            b.job([], lru2)

            def h_wo(oc, ps, pk, x=x, xkey=xkey):
                S.op('dve', lambda e: e.tensor_tensor(out=x[:, oc, :], in0=ps[:], in1=x[:, oc, :], op=ALU.add), reads=[pk, (xkey, oc)], writes=[(xkey, oc)])
            emit_linear(b, wb_out, k_out, D, KC, lambda kc: gh[:, kc, :], lambda kc: ('gh', kc), h_wo, T)
            emit_ffn(b, x, xkey, PV_AFFN, pv, hn, ones, sq, rt, wb_fin, k_fin, wb_fout, k_fout, sg, mt)

            def st(_s, _k, i=i, x=x, xkey=xkey):
                S.dma('pool', 'h1o', lambda e: e.dma_start(out=h1T[:, i * T:(i + 1) * T].rearrange("(c p) n -> p c n", p=128), in_=x[:]),
                      reads=[(xkey, c) for c in range(KC)])
            b.job([], st)
            emit_rmsnorm(b, x, xkey, T, PV_KVN, pv, hn, 'hn', ones, sq, rt, 'kv')
            for g in range(3):
                d = DIL[g]

                def h_k(oc, ps, pk, g=g, d=d, i=i, tbl=tbl):
                    hk = oc
                    emit_head_post(b, ps, pk, pv[:, PV_KN + g:PV_KN + g + 1], permK[g][:], "permK%d" % g, tbl, d, kts[:, hk, :], ('kts', hk), ones, hw)
                    if hk == 1:
                        if g < 2:
                            dst = KTo[g][:, :, 4 * i:4 * i + 4, :].rearrange("p h b k -> p h (b k)")
                            S.dma('pool', 'kto', (lambda dst: lambda e: e.dma_start(out=dst, in_=kts[:]))(dst), reads=[('kts', 0), ('kts', 1)])
                        else:
                            s_, j_ = i // 4, i % 4
                            for h2 in range(2):
                                dst = KTo[2][:, h2, 16 * s_:16 * s_ + 16, 32 * j_:32 * j_ + 32]
                                S.dma('pool', 'kto', (lambda dst, h2: lambda e: e.dma_start(out=dst, in_=kts[:, h2, :].rearrange("p (ph l) -> p ph l", ph=16)))(dst, h2),
                                      reads=[('kts', 0), ('kts', 1)])
                rhs_fn = (lambda d: (lambda kc: perm_view(hn[:, kc, :], d)))(d)
                emit_linear_perm(b, wb_kv[:, g * 512:g * 512 + 256], k_kv, 256, KC, rhs_fn, lambda kc: ('hn', kc), h_k, T, d)

                def vloads(g=g):
                    def lf(slab):
                        dst = slab[:, 0:KC * 256].rearrange("p (k n) -> p k n", k=KC)
                        return dst, wb_kv.rearrange("(k p) n -> p k n", p=128)[:, :, g * 512 + 256:g * 512 + 512], k_kv
                    return [lf]

                def vcomp(slab, skey, g=g, i=i):
                    sv = slab[:, 0:KC * 256].rearrange("p (k n) -> p k n", k=KC)
                    for blk in range(4):
                        vb = vbi[0] % 2
                        vbi[0] += 1
                        psa, pka = b.bank()
                        for kc in range(KC):
                            S.op('pe', (lambda kc, blk, psa: lambda e: e.matmul(psa[:, 0:256], lhsT=hn[:, kc, blk * 128:(blk + 1) * 128], rhs=sv[:, kc, :],
                                                                               start=(kc == 0), stop=(kc == KC - 1)))(kc, blk, psa),
                                 reads=[skey, ('hn', kc)], writes=[pka])
                        S.op('act', (lambda vb, psa: lambda e: e.activation(out=vts[:, vb, :], in_=psa[:, 0:256], func=AF.Identity))(vb, psa), reads=[pka], writes=[('vts', vb)])

                        def vdma(dst, src, vb=vb):
                            k = 'vo%d' % (voi[0] % 8)
                            voi[0] += 1
                            S.dma('pool', k, lambda e: e.dma_start(out=dst, in_=src), reads=[('vts', vb)])
                        if g == 0:
                            r0 = (4 * i + blk) * 128
                            vdma(Vo[0][r0:r0 + 128, :], vts[:, vb, :])
                        elif g == 1:
                            for p in range(4):
                                r0 = (4 * i + p) * 128 + 32 * blk
                                vdma(Vo[1][r0:r0 + 32, :], vts[p::4, vb, :])
                        else:
                            for p in range(16):
                                r0 = (16 * (i // 4) + p) * 128 + 32 * (i % 4) + 8 * blk
                                vdma(Vo[2][r0:r0 + 8, :], vts[p::16, vb, :])
                b.job(vloads(), vcomp)

        b.run_jobs()
        S.barrier_all()
        for ti, segs in enumerate(TAILS):
            tb_ = dd_['tail_b%d' % ti]
            for (kind, g, off, n) in segs:
                d = DIL[g]
                if kind == 'K':
                    src = dd_['KTh%d' % g][:, :, 32:32 + d, :]
                    dst = tb_[:, off:off + n].rearrange("p (h b k) -> p h b k", h=2, b=d)
                else:
                    src = dd_['Vh%d' % g][32 * 128:(32 + d) * 128, :].rearrange("(b k) n -> k b n", k=128)
                    dst = tb_[:, off:off + n].rearrange("p (b n) -> p b n", b=d)
                S.dma('sp', 'pk%d' % ti, (lambda dst, src: lambda e: e.dma_start(out=dst, in_=src))(dst, src))
        S.barrier_all()
        for ti in range(len(TAILS)):
            S.dma('pool', 'cc', (lambda ti: lambda e: e.collective_compute("AllGather", op=ALU.bypass, replica_groups=[[0, 1, 2, 3], [4, 5, 6, 7]],
                                                                           ins=[dd_['tail_b%d' % ti].opt()], outs=[dd_['tail_g%d' % ti].opt()]))(ti), inc=1)
        S.barrier_all()


def build_stageB(prog):
    nc = prog['nc']
    dd_ = prog['dram']
    es = ExitStack()
    with es:
        b = Bld(nc, es, pfx="b_", sem_es=prog['sem_es'], dram=prog['dram'], wcache=prog['wcache'], S=prog['S'])
        S = b.S
        h1T = dd_["h1T"]
        pvd = dd_["pvec"]
        p0d = dd_["p0"]
        mown_d = dd_["mown"]
        mprev_d = dd_["mprev"]
        KTh = [dd_["KTh%d" % g] for g in range(3)]
        Vh = [dd_["Vh%d" % g] for g in range(3)]
        outT = dd_["outT"]

        pv = b.sb("pv", [128, NPVT], F32)
        b.cload(pv[:], pvd, 'pv')
        ones = b.sb("ones", [128, 128], BF16)
        S.op('pool', lambda e: e.memset(ones[:], 1.0), writes=['ones'])
        p0 = b.sb("p0s", [32, 32], F32)
        b.cload(p0[:], p0d, 'p0')
        mown = b.sb("mown_s", [128, 4, 128], BF16)
        mprev = b.sb("mprev_s", [128, 4, 128], BF16)
        b.cload(mown[:], mown_d, 'mown')
        b.cload(mprev[:], mprev_d, 'mprev')
        wb_q, k_q = cast_weight(b, "b_w_q", None, None)
        wb_o, k_o = cast_weight(b, "b_w_o", None, None)
        wb_fin, k_fin = cast_weight(b, "b_ffn_w_in", None, None)
        wb_fout, k_fout = cast_weight(b, "b_ffn_w_out", None, None)
        ident = b.sb("ident", [128, 128], BF16)
        b.cload(ident[:], dd_["ident"], 'ident')

        if True:
            for ti, segs in enumerate(TAILS):
                tg_ = dd_['tail_g%d' % ti]
                tsel = dd_['tail_s%d' % ti]
                CH = TAILW[ti]
                acc = b.slabs[0]
                for r in range(4):
                    cand = b.slabs[1 + (r % 2)]
                    ck = ('slab', 1 + (r % 2))
                    S.dma('sp', 'hs%d' % (r % 2), (lambda cand, r, tg_, CH: lambda e: e.dma_start(out=cand[:, 0:CH], in_=tg_[r * 128:(r + 1) * 128, :]))(cand, r, tg_, CH),
                          writes=[ck])
                    selc = pv[:, PV_SEL + 4 + r:PV_SEL + 5 + r]
                    if r == 0:
                        S.op('dve', (lambda cand, selc, CH: lambda e: e.tensor_scalar(out=acc[:, 0:CH], in0=cand[:, 0:CH], scalar1=selc, scalar2=None, op0=ALU.mult))(cand, selc, CH),
                             reads=[ck, 'pv'], writes=[('slab', 0)])
                    else:
                        S.op('dve', (lambda cand, selc, CH: lambda e: e.scalar_tensor_tensor(out=acc[:, 0:CH], in0=cand[:, 0:CH], scalar=selc, in1=acc[:, 0:CH], op0=ALU.mult, op1=ALU.add))(cand, selc, CH),
                             reads=[ck, 'pv', ('slab', 0)], writes=[('slab', 0)])
                S.dma('sp', 'hso', (lambda tsel, CH: lambda e: e.dma_start(out=tsel, in_=acc[:, 0:CH]))(tsel, CH), reads=[('slab', 0)])
            S.barrier_all()
            for ti, segs in enumerate(TAILS):
                tsel = dd_['tail_s%d' % ti]
                for (kind, g, off, n) in segs:
                    d = DIL[g]
                    if kind == 'K':
                        dst = dd_['KTh%d' % g][:, :, 0:d, :]
                        src = tsel[:, off:off + n].rearrange("p (h b k) -> p h b k", h=2, b=d)
                    else:
                        dst = dd_['Vh%d' % g][0:d * 128, :].rearrange("(b k) n -> k b n", k=128)
                        src = tsel[:, off:off + n].rearrange("p (b n) -> p b n", b=d)
                    S.dma('sp', 'uk%d' % ti, (lambda dst, src: lambda e: e.dma_start(out=dst, in_=src))(dst, src))
            S.barrier_all()
        xt = [b.sb("xt%d" % i, [128, KC, T], F32) for i in range(2)]
        hn = b.sb("hn", [128, KC, T], BF16)
        sq = b.sb("sq", [128, 2, T], BF16)
        rt = b.sb("rt", [128, T], F32)
        sg = b.sb("sg", [128, 2, T], BF16)
        mt = b.sb("mt", [128, FC, T], BF16)
        Ct = [b.sb("Ct%d" % i, [32, T], F32) for i in range(2)]
        St = [b.sb("St%d" % i, [32, T], F32) for i in range(2)]
        hw = make_hw(b, pv, 2)
        permQ = [make_permG(b, p0, pv, PV_QN + g, "permQ%d" % g) for g in range(3)]
        QT = b.sb("QT", [128, 4, T], BF16)
        NUM = b.sb("NUM", [128, 4, T], F32)
        DEN = b.sb("DEN", [128, 4, T], F32)
        OT = b.sb("OT", [128, 8, T], BF16)
        KTb = [b.sb("KTs0", [128, 8, 128], BF16), b.sb("KTs1", [128, 8, 128], BF16), b.sb("KTs2", [128, 32, 128], BF16)]
        Vb = [b.sb("Vs0", [128, 8, 128], BF16), b.sb("Vs1", [128, 8, 128], BF16), b.sb("Vs2", [128, 32, 128], BF16)]
        Pt = [b.sb("Pt%d" % i, [128, 512], BF16) for i in range(4)]
        pti = [0]

        def xload(i):
            def ld(_s, _k, i=i):
                x = xt[i % 2]
                xkey = 'x%d' % (i % 2)
                S.dma('sp', xkey, lambda e: e.dma_start(out=x[:], in_=h1T[:, i * T:(i + 1) * T].rearrange("(c p) n -> p c n", p=128)),
                      writes=[(xkey, c) for c in range(KC)])
            b.job([], ld)

        xload(0)
        for i in range(NT):
            x = xt[i % 2]
            xkey = 'x%d' % (i % 2)
            if i + 1 < NT:
                xload(i + 1)
            tb = load_tables(b, dd_, i, Ct, St)
            emit_rmsnorm(b, x, xkey, T, PV_BN, pv, hn, 'hn', ones, sq, rt, 'b')

            for hk in range(2):
                def kvld(_s, _k, i=i, hk=hk):
                    for g in range(3):
                        if g == 0:
                            b0, nb = 4 * i, 5
                        elif g == 1:
                            b0, nb = 4 * i, 8
                        else:
                            b0, nb = 16 * (i // 4), 32
                        S.dma('sp', 'ktl%d' % g, (lambda g, b0, nb: lambda e: e.dma_start(out=KTb[g][:, 0:nb, :], in_=KTh[g][:, hk, b0:b0 + nb, :]))(g, b0, nb), writes=[('KTs', g)])
                        S.dma('sp', 'vl%d' % g, (lambda g, b0, nb: lambda e: e.dma_start(out=Vb[g][:, 0:nb, :], in_=Vh[g][b0 * 128:(b0 + nb) * 128, hk * 128:(hk + 1) * 128].rearrange("(b k) n -> k b n", k=128)))(g, b0, nb),
                              writes=[('Vs', g)])
                b.job([], kvld)
                for g in range(3):
                    d = DIL[g]
                    nq = 128 if g < 2 else 32
                    nqb = T // nq

                    KTs = KTb[g]
                    Vs = Vb[g]
                    kkey = ('KTs', g)
                    vkey = ('Vs', g)

                    def h_q(oc, ps, pk, g=g, d=d, tb=tb):
                        emit_head_post(b, ps, pk, pv[:, PV_QN + g:PV_QN + g + 1], permQ[g][:], "permQ%d" % g, tb, d, QT[:, oc, :], ('QT', oc), ones, hw)
                    rhs_fn = (lambda d: (lambda kc: perm_view(hn[:, kc, :], d)))(d)
                    c0 = g * 1024 + hk * 512
                    emit_linear_perm(b, wb_q[:, c0:c0 + 512], k_q, 512, KC, rhs_fn, lambda kc: ('hn', kc), h_q, T, d)

                    def attn(_s, _k, g=g, d=d, nq=nq, nqb=nqb, i=i, KTs=KTs, Vs=Vs, kkey=kkey, vkey=vkey):
                        def stA(qb):
                            if g == 0:
                                own, prev = qb + 1, qb
                                halo = (i == 0 and qb == 0)
                                lq0 = 0
                            elif g == 1:
                                own, prev = 4 + qb, qb
                                halo = (i == 0)
                                lq0 = 0
                            else:
                                own, prev = 16 + qb, qb
                                halo = (i < 4)
                                lq0 = 32 * (i % 4)
                            n4 = 4 * nq
                            rhs = QT[:, :, qb * nq:(qb + 1) * nq]
                            pts = []
                            for which, blk in (('prev', prev), ('own', own)):
                                ps, pk = b.bank()
                                mk = (mprev if which == 'prev' else mown)
                                mview = mk[:, :, lq0:lq0 + nq]
                                S.op('pe', (lambda ps, blk, rhs: lambda e: e.matmul(ps[:, 0:n4].rearrange("p (h q) -> p h q", h=4), lhsT=KTs[:, blk, :], rhs=rhs, start=True, stop=False))(ps, blk, rhs),
                                     reads=[kkey] + [('QT', hh) for hh in range(4)], writes=[pk])
                                S.op('pe', (lambda ps, mview: lambda e: e.matmul(ps[:, 0:n4].rearrange("p (h q) -> p h q", h=4), lhsT=ident[:], rhs=mview, start=False, stop=True))(ps, mview),
                                     reads=['ident', 'mown', 'mprev'], writes=[pk])
                                pt = Pt[pti[0] % 4]
                                ptk = ('Pt', pti[0] % 4)
                                pti[0] += 1
                                bias = pv[:, PV_HB:PV_HB + 1] if (which == 'prev' and halo) else pv[:, PV_ZERO:PV_ZERO + 1]
                                S.op('act', (lambda ps, pt, bias: lambda e: e.activation(out=pt[:, 0:n4], in_=ps[:, 0:n4], func=AF.Exp, scale=SCALE, bias=bias))(ps, pt, bias),
                                     reads=[pk, 'pv'], writes=[ptk])
                                pts.append((pt, ptk, blk))
                            return (qb, n4, pts)

                        def stB(st_):
                            qb, n4, pts = st_
                            psn, pkn = b.bank()
                            psd, pkd = b.bank()
                            for j, (pt, ptk, blk) in enumerate(pts):
                                S.op('pe', (lambda pt, blk, j, psn: lambda e: e.matmul(psn[:, 0:n4], lhsT=Vs[:, blk, :], rhs=pt[:, 0:n4], start=(j == 0), stop=(j == 1)))(pt, blk, j, psn),
                                     reads=[vkey, ptk], writes=[pkn])
                            for j, (pt, ptk, blk) in enumerate(pts):
                                S.op('pe', (lambda pt, j, psd: lambda e: e.matmul(psd[:, 0:n4], lhsT=ones[:], rhs=pt[:, 0:n4], start=(j == 0), stop=(j == 1)))(pt, j, psd),
                                     reads=['ones', ptk], writes=[pkd])
                            if g == 0:
                                nview = NUM[:, :, qb * 128:(qb + 1) * 128]
                                dview = DEN[:, :, qb * 128:(qb + 1) * 128]
                            else:
                                nview = NUM[:].rearrange("p h (l ph) -> p h ph l", ph=d)[:, :, qb, :]
                                dview = DEN[:].rearrange("p h (l ph) -> p h ph l", ph=d)[:, :, qb, :]
                            nkeys = [('NUM', hh) for hh in range(4)]
                            dkeys = [('DEN', hh) for hh in range(4)]
                            pn3 = psn[:, 0:n4].rearrange("p (h q) -> p h q", h=4)
                            pd3 = psd[:, 0:n4].rearrange("p (h q) -> p h q", h=4)
                            if g == 0:
                                S.op('act', (lambda pn3, nview: lambda e: e.activation(out=nview, in_=pn3, func=AF.Identity))(pn3, nview), reads=[pkn], writes=nkeys)
                                S.op('dve', (lambda pd3, dview: lambda e: e.tensor_copy(out=dview, in_=pd3))(pd3, dview), reads=[pkd], writes=dkeys)
                            else:
                                S.op('dve', (lambda pn3, nview: lambda e: e.tensor_tensor(out=nview, in0=pn3, in1=nview, op=ALU.add))(pn3, nview), reads=[pkn] + nkeys, writes=nkeys)
                                S.op('dve', (lambda pd3, dview: lambda e: e.tensor_tensor(out=dview, in0=pd3, in1=dview, op=ALU.add))(pd3, dview), reads=[pkd] + dkeys, writes=dkeys)

                        prev_st = stA(0)
                        for qb in range(1, nqb):
                            nxt = stA(qb)
                            stB(prev_st)
                            prev_st = nxt
                        stB(prev_st)
                    b.job([], attn)

                def fin_attn(_s, _k, hk=hk):
                    allk = [('DEN', h) for h in range(4)]
                    S.op('dve', lambda e: e.reciprocal(out=DEN[:], in_=DEN[:]), reads=allk, writes=allk)
                    for h in range(4):
                        S.op('dve', (lambda h: lambda e: e.tensor_tensor(out=OT[:, 4 * hk + h, :], in0=NUM[:, h, :], in1=DEN[:, h, :], op=ALU.mult))(h),
                             reads=[('NUM', h), ('DEN', h)], writes=[('OT', 4 * hk + h)])
                b.job([], fin_attn)

            def h_wo(oc, ps, pk, x=x, xkey=xkey):
                S.op('dve', lambda e: e.tensor_tensor(out=x[:, oc, :], in0=ps[:], in1=x[:, oc, :], op=ALU.add), reads=[pk, (xkey, oc)], writes=[(xkey, oc)])
            emit_linear(b, wb_o, k_o, D, KC, lambda kc: OT[:, kc, :], lambda kc: ('OT', kc), h_wo, T)
            emit_ffn(b, x, xkey, PV_BFFN, pv, hn, ones, sq, rt, wb_fin, k_fin, wb_fout, k_fout, sg, mt)

            def st(_s, _k, i=i, x=x, xkey=xkey):
                S.dma('pool', 'oo', lambda e: e.dma_start(out=outT[:, i * T:(i + 1) * T].rearrange("(c p) n -> p c n", p=128), in_=x[:]),
                      reads=[(xkey, c) for c in range(KC)])
            b.job([], st)

        b.run_jobs()
        S.barrier_all()


TAILS = [[('K', 0, 0, 256), ('K', 1, 256, 1024), ('V', 0, 1280, 256), ('V', 1, 1536, 1024)],
         [('K', 2, 0, 4096)],
         [('V', 2, 0, 4096)]]
TAILW = [2560, 4096, 4096]


def build_fused():
    nc = bass.Bass("TRN2", target_bir_lowering=False)
    sem_es = ExitStack()
    with sem_es:
        dram = {}

        def ein(name, shape, dt):
            dram[name] = nc.dram_tensor(name, list(shape), dt, kind="ExternalInput").ap()

        ein("xT", [D, S_CORE], F32)
        ein("xh", [D, 4], F32)
        ein("pvec", [128, NPVT], F32)
        ein("a_w_in", [D, 2 * D], F32)
        ein("gaw", [128, 8, 128], F32)
        ein("gxw", [128, 8, 128], F32)
        ein("a_w_out", [D, D], F32)
        ein("a_ffn_w_in", [D, 2 * FFN], F32)
        ein("a_ffn_w_out", [FFN, D], F32)
        ein("w_kv", [D, 1536], F32)
        ein("pos32", [32, S_CORE], I32)
        ein("p0", [32, 32], F32)
        ein("b_w_q", [D, 3072], F32)
        ein("b_w_o", [D, D], F32)
        ein("b_ffn_w_in", [D, 2 * FFN], F32)
        ein("b_ffn_w_out", [FFN, D], F32)
        ein("ident", [128, 128], BF16)
        ein("mown", [128, 4, 128], BF16)
        ein("mprev", [128, 4, 128], BF16)
        dram["outT"] = nc.dram_tensor("outT", [D, S_CORE], F32, kind="ExternalOutput").ap()
        dram["AB"] = nc.dram_tensor("ab_bounce", [128, 16], F32).ap()
        dram["ABg"] = nc.dram_tensor("ab_gath", [4 * 128, 16], F32).ap()
        dram["ABall"] = dram["ABg"].rearrange("(r p) n -> p r n", p=128)
        dram["h1T"] = nc.dram_tensor("h1_spill", [D, S_CORE], F32).ap()
        dram["asp"] = nc.dram_tensor("a_spill", [128, NT, KC, T], F32).ap()
        dram["usp"] = nc.dram_tensor("u_spill", [128, NT, KC, T], F32).ap()
        dram["ctab"] = nc.dram_tensor("ctab", [32, S_CORE], F32).ap()
        dram["stab"] = nc.dram_tensor("stab", [32, S_CORE], F32).ap()
        for g in range(3):
            d = DIL[g]
            dram["KTh%d" % g] = nc.dram_tensor("kth%d" % g, [128, 2, 32 + d, 128], BF16).ap()
            dram["Vh%d" % g] = nc.dram_tensor("vh%d" % g, [(32 + d) * 128, 256], BF16).ap()
            dram["KT%d" % g] = dram["KTh%d" % g][:, :, d:d + 32, :]
            dram["V%d" % g] = dram["Vh%d" % g][d * 128:(d + 32) * 128, :]
        for ti in range(len(TAILS)):
            dram["tail_b%d" % ti] = nc.dram_tensor("tail_b%d" % ti, [128, TAILW[ti]], BF16).ap()
            dram["tail_g%d" % ti] = nc.dram_tensor("tail_g%d" % ti, [4 * 128, TAILW[ti]], BF16).ap()
            dram["tail_s%d" % ti] = nc.dram_tensor("tail_s%d" % ti, [128, TAILW[ti]], BF16).ap()
        S = Sched(nc, sem_es, "")
        prog = {'nc': nc, 'sem_es': sem_es, 'dram': dram, 'wcache': {}, 'S': S}
        stages = os.environ.get("STAGES", "123")
        build_stageA(1, prog)
        if "2" in stages:
            build_stageA(2, prog)
        if "3" in stages:
            build_stageB(prog)
        S.emit()
    return nc


def fm(v):
    return np.ascontiguousarray(np.asarray(v, np.float32).reshape(8, 128).T)


def make_pvec(inp, core):
    pvn = np.zeros((128, NPVT), np.float32)
    pvn[:, PV_ANORM:PV_ANORM + 8] = fm(inp["a_norm"][0])
    for k in range(4):
        pvn[:, PV_CONVW + 8 * k:PV_CONVW + 8 * k + 8] = fm(inp["a_conv_w"][0, k])
    pvn[:, PV_CONVB:PV_CONVB + 8] = fm(inp["a_conv_b"][0])
    pvn[:, PV_GAB:PV_GAB + 8] = fm(inp["a_gate_a_b"][0])
    pvn[:, PV_GXB:PV_GXB + 8] = fm(inp["a_gate_x_b"][0])
    pvn[:, PV_LAM:PV_LAM + 8] = fm(inp["a_lambda"][0])
    pvn[:, PV_AFFN:PV_AFFN + 8] = fm(inp["a_ffn_norm"][0])
    pvn[:, PV_KVN:PV_KVN + 8] = fm(inp["kv_norm"])
    pvn[:, PV_BN:PV_BN + 8] = fm(inp["b_norm"][0])
    pvn[:, PV_BFFN:PV_BFFN + 8] = fm(inp["b_ffn_norm"][0])
    pvn[:, PV_KN:PV_KN + 3] = np.asarray(inp["k_norm"], np.float32).T
    pvn[:, PV_QN:PV_QN + 3] = np.asarray(inp["b_q_norm"][0], np.float32).T
    invf = (500000.0 ** (-np.arange(0, 32, 2, dtype=np.float32) / 32)).astype(np.float32)
    pvn[0:32, PV_INVF] = np.concatenate([invf, invf])
    bseq, j = core // 4, core % 4
    for r in range(4):
        pvn[:, PV_SEL + r] = 1.0 if r < j else 0.0
        pvn[:, PV_SEL + 4 + r] = 1.0 if r == j - 1 else 0.0
    pvn[:, PV_HB] = -30000.0 if j == 0 else 0.0
    pvn[:, PV_EPS] = EPS
    pvn[:, PV_ONE] = 1.0
    return pvn


def kernel(**inputs):
    inp = {k: np.asarray(v) for k, v in inputs.items()}
    x = inp["x"].astype(np.float32, copy=False)
    cores = list(range(NCORES))
    xTs, xhs, poss = [], [], []
    for c in cores:
        bseq, j = c // 4, c % 4
        xc = x[bseq, j * S_CORE:(j + 1) * S_CORE, :]
        xTs.append(np.ascontiguousarray(xc.T))
        if j == 0:
            xhs.append(np.zeros((D, 4), np.float32))
        else:
            xhs.append(np.ascontiguousarray(x[bseq, j * S_CORE - 4:j * S_CORE, :].T))
        p = inp["positions"][bseq, j * S_CORE:(j + 1) * S_CORE].astype(np.int32)
        poss.append(np.ascontiguousarray(np.broadcast_to(p[None, :], (32, S_CORE))))
    pvecs = [make_pvec(inp, c) for c in cores]
    a_w_in = np.ascontiguousarray(inp["a_w_in"][0], np.float32)
    gaw = np.ascontiguousarray(np.transpose(inp["a_gate_a_w"][0], (1, 0, 2)), np.float32)
    gxw = np.ascontiguousarray(np.transpose(inp["a_gate_x_w"][0], (1, 0, 2)), np.float32)
    p0 = np.zeros((32, 32), np.float32)
    for m in range(16):
        p0[m + 16, m] = -1.0
    for m in range(16, 32):
        p0[m - 16, m] = 1.0
    li = np.arange(128)
    mown = np.ascontiguousarray(np.broadcast_to(np.where(li[:, None] <= li[None, :], 0.0, -30000.0)[:, None, :], (128, 4, 128))).astype(ml_dtypes.bfloat16)
    mprev = np.ascontiguousarray(np.broadcast_to(np.where(li[:, None] >= li[None, :], 0.0, -30000.0)[:, None, :], (128, 4, 128))).astype(ml_dtypes.bfloat16)
    ident = np.eye(128, dtype=np.float32).astype(ml_dtypes.bfloat16)

    nc = build_fused()
    w = lambda k: np.ascontiguousarray(inp[k][0], np.float32)
    shared = {"a_w_in": a_w_in, "gaw": gaw, "gxw": gxw, "a_w_out": w("a_w_out"), "a_ffn_w_in": w("a_ffn_w_in"),
              "a_ffn_w_out": w("a_ffn_w_out"), "w_kv": np.ascontiguousarray(inp["w_kv"], np.float32), "p0": p0,
              "b_w_q": w("b_w_q"), "b_w_o": w("b_w_o"), "b_ffn_w_in": w("b_ffn_w_in"), "b_ffn_w_out": w("b_ffn_w_out"),
              "mown": mown, "mprev": mprev, "ident": ident}
    in_maps = []
    for c in cores:
        m = dict(shared)
        m.update({"xT": xTs[c], "xh": xhs[c], "pvec": pvecs[c], "pos32": poss[c]})
        in_maps.append(m)
    res = run_bass_kernel_spmd(nc, in_maps, core_ids=cores).results
    out = np.empty((2, 4 * S_CORE, D), np.float32)
    for c in cores:
        bseq, j = c // 4, c % 4
        out[bseq, j * S_CORE:(j + 1) * S_CORE, :] = res[c]["outT"].T
    return out
```

```python
import math
import os
import numpy as np
import ml_dtypes
from contextlib import ExitStack
import concourse.bass as bass
import concourse.mybir as mybir
from concourse.bass_utils import run_bass_kernel_spmd

F32 = mybir.dt.float32
BF16 = mybir.dt.bfloat16
I32 = mybir.dt.int32
AF = mybir.ActivationFunctionType
ALU = mybir.AluOpType
AX = mybir.AxisListType

NCORES = 8
D = 1024
KC = 8
S_CORE = 4096
T = 512
NT = S_CORE // T
FFN = 2816
FC = FFN // 128
EPS = 1e-6
DIL = (1, 4, 16)
TWO_PI = 2.0 * math.pi
CW1 = 6.28125
CW2 = TWO_PI - CW1
SCALE = 128.0 ** -0.5
NSLAB = 4

PV_ANORM, PV_CONVW, PV_CONVB, PV_GAB, PV_GXB, PV_LAM, PV_AFFN, PV_KVN, PV_BN, PV_BFFN, PV_KN, PV_QN, PV_INVF, PV_SEL, PV_HB = \
    0, 8, 40, 48, 56, 64, 72, 80, 88, 96, 104, 107, 110, 111, 119
NPV = 120
PV_EPS, PV_ONE, PV_ZERO = 120, 121, 122
NPVT = 123


class Sched:
    ENGS = ('pe', 'act', 'dve', 'pool', 'sp')

    def __init__(self, nc, es, pfx=""):
        self.nc = nc
        self.es = es
        self.pfx = pfx
        self.q = {e: [] for e in self.ENGS}
        self.cnt = {e: 0 for e in self.ENGS}
        self.sem = {e: es.enter_context(nc.semaphore(pfx + "s_" + e)) for e in self.ENGS if e != 'sp'}
        self.dsem = {}
        self.dcnt = {}
        self.last_w = {}
        self.readers = {}
        self.seen = {e: {} for e in self.ENGS}

    def _dma_sem(self, key):
        if key not in self.dsem:
            self.dsem[key] = self.es.enter_context(self.nc.semaphore(self.pfx + "d_" + key))
            self.dcnt[key] = 0
        return self.dsem[key]

    def _deps(self, reads, writes):
        toks = set()
        for k in list(reads) + list(writes):
            t = self.last_w.get(k)
            if t is not None:
                toks.add(t)
        for k in writes:
            for t in self.readers.get(k, ()):
                toks.add(t)
        return toks

    def _record(self, tok, reads, writes):
        for k in reads:
            self.readers.setdefault(k, []).append(tok)
        for k in writes:
            self.last_w[k] = tok
            self.readers[k] = []

    def _waits(self, eng, toks):
        need = {}
        for t in toks:
            if t[0] == 'c':
                _, e, idx = t
                if e == eng and e == 'pe':
                    continue
                key = ('c', e)
                need[key] = max(need.get(key, 0), idx)
            else:
                _, k, n = t
                key = ('d', k)
                need[key] = max(need.get(key, 0), n)
        out = []
        for key, val in need.items():
            if self.seen[eng].get(key, 0) >= val:
                continue
            self.seen[eng][key] = val
            sem = self.sem[key[1]] if key[0] == 'c' else self.dsem[key[1]]
            out.append((sem, val))
        return out

    @staticmethod
    def _excl(reads, writes):
        ps = [k for k in reads if isinstance(k, tuple) and k[0] == 'ps']
        if ps:
            reads = [k for k in reads if not (isinstance(k, tuple) and k[0] == 'ps')]
            writes = list(writes) + ps
        return reads, writes

    def op(self, eng, fn, reads=(), writes=()):
        reads, writes = self._excl(reads, writes)
        toks = self._deps(reads, writes)
        waits = self._waits(eng, toks)
        self.cnt[eng] += 1
        tok = ('c', eng, self.cnt[eng])
        self.q[eng].append((waits, fn, self.sem[eng], 1))
        self._record(tok, reads, writes)
        return tok

    def dma(self, eng, key, fn, reads=(), writes=(), serialize=True, inc=16):
        sem = self._dma_sem(key)
        toks = self._deps(reads, writes)
        if serialize and self.dcnt[key] > 0:
            toks.add(('d', key, self.dcnt[key]))
        waits = self._waits(eng, toks)
        self.dcnt[key] += inc
        tok = ('d', key, self.dcnt[key])
        self.q[eng].append((waits, fn, sem, inc))
        self._record(tok, reads, writes)
        return tok

    def finish(self, eng='sp'):
        waits = []
        for k, n in self.dcnt.items():
            if n > 0:
                waits.append((self.dsem[k], n))
        for e in self.ENGS:
            if e != 'sp' and e != eng and self.cnt[e] > 0:
                waits.append((self.sem[e], self.cnt[e]))
        self.q[eng].append((waits, None, None, 0))

    def barrier_all(self, skip=()):
        for eng in self.ENGS:
            waits = []
            for k, n in self.dcnt.items():
                if k in skip:
                    continue
                if n > 0 and self.seen[eng].get(('d', k), 0) < n:
                    self.seen[eng][('d', k)] = n
                    waits.append((self.dsem[k], n))
            for e in self.ENGS:
                if e != 'sp' and e != eng and self.cnt[e] > 0 and self.seen[eng].get(('c', e), 0) < self.cnt[e]:
                    self.seen[eng][('c', e)] = self.cnt[e]
                    waits.append((self.sem[e], self.cnt[e]))
            self.q[eng].append((waits, None, None, 0))

    def emit(self):
        nc = self.nc
        q = self.q

        def run(engobj, lst):
            for waits, fn, sem, inc in lst:
                for s, v in waits:
                    engobj.wait_ge(s, v)
                if fn is not None:
                    ins = fn(engobj)
                    ins.then_inc(sem, inc)

        with nc.Block() as block:
            @block.tensor
            def _(e):
                run(e, q['pe'])

            @block.scalar
            def _(e):
                run(e, q['act'])

            @block.vector
            def _(e):
                run(e, q['dve'])

            @block.gpsimd
            def _(e):
                run(e, q['pool'])

            @block.sync
            def _(e):
                run(e, q['sp'])


class Bld:
    def __init__(self, nc, es, pfx="", sem_es=None, dram=None, wcache=None, S=None):
        self.nc = nc
        self.es = es
        self.pfx = pfx
        self.dram = dram if dram is not None else {}
        self.wcache = wcache if wcache is not None else {}
        self.fused = dram is not None
        self.S = S if S is not None else Sched(nc, sem_es if sem_es is not None else es, pfx)
        self.banks = [es.enter_context(nc.psum_tensor(pfx + "psb%d" % i, [128, 512], F32)) for i in range(8)]
        self.bi = 0
        self.slabs = [self.sb("slab%d" % i, [128, 4096], BF16) for i in range(NSLAB)]
        self.si = 0
        self.jobs = []
        self.ci = 0

    def sb(self, name, shape, dt):
        return self.es.enter_context(self.nc.sbuf_tensor(self.pfx + name, shape, dt))

    def bank(self):
        i = self.bi
        self.bi = (self.bi + 1) % 8
        return self.banks[i], ('ps', i)

    def dram_in(self, name, shape, dt):
        if name in self.dram:
            return self.dram[name]
        return self.nc.dram_tensor(name, list(shape), dt, kind="ExternalInput").ap()

    def dram_out(self, name, shape, dt):
        if name in self.dram:
            return self.dram[name]
        return self.nc.dram_tensor(name, list(shape), dt, kind="ExternalOutput").ap()

    def dram_tmp(self, name, shape, dt):
        return self.nc.dram_tensor(name, list(shape), dt).ap()

    def cload(self, dst, src, wkey, eng='sp'):
        k = 'c%d' % (self.ci % 4)
        self.ci += 1
        wk = wkey if isinstance(wkey, list) else [wkey]
        self.S.dma(eng, k, lambda e: e.dma_start(out=dst, in_=src), writes=wk)

    def job(self, loads, compute):
        self.jobs.append((loads, compute))

    def run_jobs(self, lookahead=2):
        S = self.S
        jobs = self.jobs
        self.jobs = []
        assigned = {}

        def do_load(j):
            loads, _ = jobs[j]
            if not loads:
                return
            si = self.si
            self.si = (self.si + 1) % NSLAB
            assigned[j] = si
            slab = self.slabs[si]
            for lf in loads:
                dst, src, skey = lf(slab)
                S.dma('sp', 'w%d' % si, (lambda dst, src: lambda e: e.dma_start(out=dst, in_=src))(dst, src),
                      reads=list(skey), writes=[('slab', si)])

        load_idx = [j for j in range(len(jobs)) if jobs[j][0]]
        ptr = 0
        for j in range(len(jobs)):
            while ptr < len(load_idx):
                ahead = sum(1 for x in load_idx[max(0, ptr - 8):ptr] if x > j)
                if load_idx[ptr] <= j or ahead < lookahead:
                    do_load(load_idx[ptr])
                    ptr += 1
                else:
                    break
            _, compute = jobs[j]
            if j in assigned:
                compute(self.slabs[assigned[j]], ('slab', assigned[j]))
            else:
                compute(None, None)


def emit_rmsnorm(b, xt, xkey, n, gcol, pv, hn, hnkey, ones, sq, rt, tag, inv_n=1.0 / D):
    S = b.S

    def compute(_s, _k):
        ps, pk = b.bank()
        for c in range(KC):
            S.op('act', (lambda c: lambda e: e.activation(out=sq[:, c % 2, 0:n], in_=xt[:, c, 0:n], func=AF.Square))(c),
                 reads=[(xkey, c)], writes=[('sq', c % 2)])
            S.op('pe', (lambda c: lambda e: e.matmul(ps[:, 0:n], lhsT=ones[:], rhs=sq[:, c % 2, 0:n], start=(c == 0), stop=(c == KC - 1)))(c),
                 reads=[('sq', c % 2), 'ones'], writes=[pk])
        S.op('act', lambda e: e.activation(out=rt[:, 0:n], in_=ps[:, 0:n], func=AF.Sqrt, scale=inv_n, bias=pv[:, PV_EPS:PV_EPS + 1]),
             reads=[pk, 'pv'], writes=['rt'])
        S.op('dve', lambda e: e.reciprocal(out=rt[:, 0:n], in_=rt[:, 0:n]), reads=['rt'], writes=['rt'])
        for c in range(KC):
            S.op('dve', (lambda c: lambda e: e.scalar_tensor_tensor(out=hn[:, c, 0:n], in0=xt[:, c, 0:n], scalar=pv[:, gcol + c:gcol + c + 1],
                                                                     in1=rt[:, 0:n], op0=ALU.mult, op1=ALU.mult))(c),
                 reads=[(xkey, c), 'rt', 'pv'], writes=[(hnkey, c)])
    b.job([], compute)


def emit_linear(b, wsrc, wkey, ncols_total, kchunks, rhs_fn, rhs_keys, handler, n, col_groups=None):
    S = b.S
    wv = wsrc.rearrange("(k p) n -> p k n", p=128)
    if col_groups is None:
        per = 4096 // kchunks // 128 * 128
        per = min(per, 512)
        col_groups = []
        c0 = 0
        while c0 < ncols_total:
            w = min(per, ncols_total - c0)
            col_groups.append([(c0, w)])
            c0 += w
    for segs in col_groups:
        width = sum(w for _, w in segs)

        def mk_loads(segs=segs, width=width):
            loads = []
            off = 0
            for (c0, w) in segs:
                def lf(slab, c0=c0, w=w, off=off):
                    dst = slab[:, 0:kchunks * width].rearrange("p (k n) -> p k n", k=kchunks)[:, :, off:off + w]
                    return dst, wv[:, :, c0:c0 + w], wkey
                loads.append(lf)
                off += w
            return loads

        def compute(slab, skey, segs=segs, width=width):
            sv = slab[:, 0:kchunks * width].rearrange("p (k n) -> p k n", k=kchunks)
            off = 0
            for (c0, w) in segs:
                for o in range(w // 128):
                    ps, pk = b.bank()
                    for kc in range(kchunks):
                        S.op('pe', (lambda kc, o, off, ps: lambda e: e.matmul(ps[:, 0:n], lhsT=sv[:, kc, off + o * 128:off + (o + 1) * 128],
                                                                                rhs=rhs_fn(kc), start=(kc == 0), stop=(kc == kchunks - 1)))(kc, o, off, ps),
                             reads=[skey, rhs_keys(kc)], writes=[pk])
                    handler((c0 + o * 128) // 128, ps, pk)
                off += w
        b.job(mk_loads(), compute)


def emit_linear_perm(b, wsrc, wkey, ncols_total, kchunks, rhs_fn, rhs_keys, handler, n, d):
    S = b.S
    wv = wsrc.rearrange("(k p) n -> p k n", p=128)
    width = ncols_total

    def lf(slab):
        dst = slab[:, 0:kchunks * width].rearrange("p (k n) -> p k n", k=kchunks)
        return dst, wv, wkey

    def compute(slab, skey):
        sv = slab[:, 0:kchunks * width].rearrange("p (k n) -> p k n", k=kchunks)
        for o in range(width // 128):
            ps, pk = b.bank()
            out = ps[:, 0:n] if d == 1 else ps[:, 0:n].rearrange("p (ph l) -> p ph l", ph=d)
            for kc in range(kchunks):
                S.op('pe', (lambda kc, o, out: lambda e: e.matmul(out, lhsT=sv[:, kc, o * 128:(o + 1) * 128], rhs=rhs_fn(kc),
                                                                   start=(kc == 0), stop=(kc == kchunks - 1)))(kc, o, out),
                     reads=[skey, rhs_keys(kc)], writes=[pk])
            handler(o, ps, pk)
    b.job([lf], compute)


def emit_ffn(b, xt, xkey, gcol, pv, hn, ones, sq, rt, w_in, w_in_key, w_out, w_out_key, sg, mt):
    S = b.S
    emit_rmsnorm(b, xt, xkey, T, gcol, pv, hn, 'hn', ones, sq, rt, 'ffn')
    groups = []
    for c in range(0, FC, 2):
        groups.append([(c * 128, 256), (FFN + c * 128, 256)])

    def h_in(oc, ps, pk):
        if oc < FC:
            S.op('act', lambda e: e.activation(out=sg[:, oc % 2, :], in_=ps[:], func=AF.Silu), reads=[pk], writes=[('sg', oc % 2)])
        else:
            c = oc - FC
            S.op('dve', lambda e: e.tensor_tensor(out=mt[:, c, :], in0=ps[:], in1=sg[:, c % 2, :], op=ALU.mult),
                 reads=[pk, ('sg', c % 2)], writes=[('mt', c)])
    emit_linear(b, w_in, w_in_key, 2 * FFN, KC, lambda kc: hn[:, kc, :], lambda kc: ('hn', kc), h_in, T, col_groups=groups)

    def h_out(oc, ps, pk):
        S.op('dve', lambda e: e.tensor_tensor(out=xt[:, oc, 0:T], in0=ps[:], in1=xt[:, oc, 0:T], op=ALU.add),
             reads=[pk, (xkey, oc)], writes=[(xkey, oc)])
    emit_linear(b, w_out, w_out_key, D, FC, lambda kc: mt[:, kc, :], lambda kc: ('mt', kc), h_out, T,
                col_groups=[[(c * 128, 128)] for c in range(KC)])


def emit_rope_tables(b, pos32, pv, i, tb):
    S = b.S
    pi, pf, an, kf, ki, m, C, Sn, tmp = tb['pi'], tb['pf'], tb['an'], tb['kf'], tb['ki'], tb['m'], tb['C'], tb['S'], tb['tmp']

    def compute(_s, _k):
        S.dma('sp', 'pos', lambda e: e.dma_start(out=pi[:], in_=pos32[:, i * T:(i + 1) * T]), writes=['pi'])
        S.op('pool', lambda e: e.tensor_copy(out=pf[:], in_=pi[:]), reads=['pi'], writes=['pf'])
        S.op('pool', lambda e: e.tensor_scalar(out=an[:], in0=pf[:], scalar1=pv[0:32, PV_INVF:PV_INVF + 1], scalar2=None, op0=ALU.mult),
             reads=['pf', 'pv'], writes=['an'])
        S.op('pool', lambda e: e.tensor_scalar(out=kf[:], in0=an[:], scalar1=1.0 / TWO_PI, scalar2=None, op0=ALU.mult), reads=['an'], writes=['kf'])
        S.op('pool', lambda e: e.tensor_copy(out=ki[:], in_=kf[:]), reads=['kf'], writes=['ki'])
        S.op('pool', lambda e: e.tensor_copy(out=kf[:], in_=ki[:]), reads=['ki'], writes=['kf'])
        S.op('pool', lambda e: e.tensor_scalar(out=tmp[:], in0=kf[:], scalar1=-CW1, scalar2=None, op0=ALU.mult), reads=['kf'], writes=['tmp'])
        S.op('pool', lambda e: e.tensor_tensor(out=an[:], in0=an[:], in1=tmp[:], op=ALU.add), reads=['tmp', 'an'], writes=['an'])
        S.op('pool', lambda e: e.tensor_scalar(out=tmp[:], in0=kf[:], scalar1=-CW2, scalar2=None, op0=ALU.mult), reads=['kf'], writes=['tmp'])
        S.op('pool', lambda e: e.tensor_tensor(out=an[:], in0=an[:], in1=tmp[:], op=ALU.add), reads=['tmp', 'an'], writes=['an'])
        S.op('pool', lambda e: e.tensor_scalar(out=m[:], in0=an[:], scalar1=math.pi, scalar2=None, op0=ALU.is_gt), reads=['an'], writes=['m'])
        S.op('pool', lambda e: e.tensor_scalar(out=tmp[:], in0=m[:], scalar1=-TWO_PI, scalar2=None, op0=ALU.mult), reads=['m'], writes=['tmp'])
        S.op('pool', lambda e: e.tensor_tensor(out=an[:], in0=an[:], in1=tmp[:], op=ALU.add), reads=['tmp', 'an'], writes=['an'])
        S.op('pool', lambda e: e.tensor_scalar(out=m[:], in0=an[:], scalar1=-math.pi, scalar2=None, op0=ALU.is_lt), reads=['an'], writes=['m'])
        S.op('pool', lambda e: e.tensor_scalar(out=tmp[:], in0=m[:], scalar1=TWO_PI, scalar2=None, op0=ALU.mult), reads=['m'], writes=['tmp'])
        S.op('pool', lambda e: e.tensor_tensor(out=an[:], in0=an[:], in1=tmp[:], op=ALU.add), reads=['tmp', 'an'], writes=['an'])
        S.op('act', lambda e: e.activation(out=Sn[:], in_=an[:], func=AF.Sin), reads=['an'], writes=['S'])
        S.op('pool', lambda e: e.tensor_scalar(out=pf[:], in0=an[:], scalar1=math.pi / 2, scalar2=None, op0=ALU.add), reads=['an'], writes=['pf'])
        S.op('pool', lambda e: e.tensor_scalar(out=m[:], in0=pf[:], scalar1=math.pi, scalar2=None, op0=ALU.is_gt), reads=['pf'], writes=['m'])
        S.op('pool', lambda e: e.tensor_scalar(out=tmp[:], in0=m[:], scalar1=-TWO_PI, scalar2=None, op0=ALU.mult), reads=['m'], writes=['tmp'])
        S.op('pool', lambda e: e.tensor_tensor(out=pf[:], in0=pf[:], in1=tmp[:], op=ALU.add), reads=['tmp', 'pf'], writes=['pf'])
        S.op('act', lambda e: e.activation(out=C[:], in_=pf[:], func=AF.Sin), reads=['pf'], writes=['C'])
    b.job([], compute)


def emit_rope_tables_all(b, pos32, pv, dd_):
    S = b.S
    PW = 1024
    pi = b.sb("tq_pi", [32, PW], I32)
    pf = b.sb("tq_pf", [32, PW], F32)
    an = b.sb("tq_an", [32, PW], F32)
    kf = b.sb("tq_kf", [32, PW], F32)
    m = b.sb("tq_m", [32, PW], F32)
    C = b.sb("tq_C", [32, PW], F32)
    Sn = b.sb("tq_S", [32, PW], F32)
    for pc in range(S_CORE // PW):
        sl = slice(pc * PW, (pc + 1) * PW)
        S.dma('sp', 'pos', (lambda sl: lambda e: e.dma_start(out=pi[:], in_=pos32[:, sl]))(sl), writes=['tq_pi'])
        S.op('dve', lambda e: e.tensor_copy(out=pf[:], in_=pi[:]), reads=['tq_pi'], writes=['tq_pf'])
        S.op('dve', lambda e: e.tensor_scalar(out=an[:], in0=pf[:], scalar1=pv[0:32, PV_INVF:PV_INVF + 1], scalar2=None, op0=ALU.mult), reads=['tq_pf', 'pv'], writes=['tq_an'])
        S.op('dve', lambda e: e.tensor_scalar(out=kf[:], in0=an[:], scalar1=1.0 / TWO_PI, scalar2=None, op0=ALU.mult), reads=['tq_an'], writes=['tq_kf'])
        S.op('dve', lambda e: e.tensor_copy(out=pi[:], in_=kf[:]), reads=['tq_kf', 'tq_pf'], writes=['tq_pi'])
        S.op('dve', lambda e: e.tensor_copy(out=kf[:], in_=pi[:]), reads=['tq_pi'], writes=['tq_kf'])
        S.op('dve', lambda e: e.scalar_tensor_tensor(out=an[:], in0=kf[:], scalar=-CW1, in1=an[:], op0=ALU.mult, op1=ALU.add), reads=['tq_kf', 'tq_an'], writes=['tq_an'])
        S.op('dve', lambda e: e.scalar_tensor_tensor(out=an[:], in0=kf[:], scalar=-CW2, in1=an[:], op0=ALU.mult, op1=ALU.add), reads=['tq_kf', 'tq_an'], writes=['tq_an'])
        S.op('dve', lambda e: e.tensor_scalar(out=m[:], in0=an[:], scalar1=math.pi, scalar2=None, op0=ALU.is_gt), reads=['tq_an'], writes=['tq_m'])
        S.op('dve', lambda e: e.scalar_tensor_tensor(out=an[:], in0=m[:], scalar=-TWO_PI, in1=an[:], op0=ALU.mult, op1=ALU.add), reads=['tq_m', 'tq_an'], writes=['tq_an'])
        S.op('dve', lambda e: e.tensor_scalar(out=m[:], in0=an[:], scalar1=-math.pi, scalar2=None, op0=ALU.is_lt), reads=['tq_an'], writes=['tq_m'])
        S.op('dve', lambda e: e.scalar_tensor_tensor(out=an[:], in0=m[:], scalar=TWO_PI, in1=an[:], op0=ALU.mult, op1=ALU.add), reads=['tq_m', 'tq_an'], writes=['tq_an'])
        S.op('act', lambda e: e.activation(out=Sn[:], in_=an[:], func=AF.Sin), reads=['tq_an'], writes=['tq_S'])
        S.op('dve', lambda e: e.tensor_scalar(out=pf[:], in0=an[:], scalar1=math.pi / 2, scalar2=None, op0=ALU.add), reads=['tq_an'], writes=['tq_pf'])
        S.op('dve', lambda e: e.tensor_scalar(out=m[:], in0=pf[:], scalar1=math.pi, scalar2=None, op0=ALU.is_gt), reads=['tq_pf'], writes=['tq_m'])
        S.op('dve', lambda e: e.scalar_tensor_tensor(out=pf[:], in0=m[:], scalar=-TWO_PI, in1=pf[:], op0=ALU.mult, op1=ALU.add), reads=['tq_m', 'tq_pf'], writes=['tq_pf'])
        S.op('act', lambda e: e.activation(out=C[:], in_=pf[:], func=AF.Sin), reads=['tq_pf'], writes=['tq_C'])
        S.dma('sp', 'cto', (lambda sl: lambda e: e.dma_start(out=dd_['ctab'][:, sl], in_=C[:]))(sl), reads=['tq_C'])
        S.dma('sp', 'sto', (lambda sl: lambda e: e.dma_start(out=dd_['stab'][:, sl], in_=Sn[:]))(sl), reads=['tq_S'])


def perm_view(ap2d, d):
    if d == 1:
        return ap2d
    return ap2d.rearrange("p (l ph) -> p ph l", ph=d)


def emit_head_post(b, ps, pk, gcol_ap, permG, pgkey, tb, d, dst, dkey, ones, hw):
    S = b.S
    n = hw['i'][0] % len(hw['sets'])
    hw['i'][0] += 1
    st = hw['sets'][n]
    sqh, x32, rth, t1, t2 = st['sq'], st['x32'], st['rt'], st['t1'], st['t2']
    ksq, kx, krt, k1, k2 = ('hsq', n), ('hx32', n), ('hrt', n), ('ht1', n), ('ht2', n)
    Cv = perm_view(tb['C'][:], d)
    Sv = perm_view(tb['S'][:], d)
    pv3 = (lambda a: a) if d == 1 else (lambda a: a.rearrange("p (ph l) -> p ph l", ph=d))
    S.op('act', lambda e: e.activation(out=sqh[:], in_=ps[:], func=AF.Square), reads=[pk], writes=[ksq])
    S.op('act', lambda e: e.activation(out=x32[:], in_=ps[0:32, :], func=AF.Identity), reads=[pk], writes=[kx])
    ps2, pk2 = b.bank()
    S.op('pe', lambda e: e.matmul(ps2[:], lhsT=ones[:], rhs=sqh[:], start=True, stop=True), reads=[ksq, 'ones'], writes=[pk2])
    ps3, pk3 = b.bank()
    S.op('pe', lambda e: e.matmul(ps3[0:32, :], lhsT=permG, rhs=x32[:], start=True, stop=True), reads=[kx, pgkey], writes=[pk3])
    S.op('act', lambda e: e.activation(out=rth[:], in_=ps2[:], func=AF.Sqrt, scale=1.0 / 128, bias=hw['eps']), reads=[pk2, 'pv'], writes=[krt])
    S.op('dve', lambda e: e.reciprocal(out=rth[:], in_=rth[:]), reads=[krt], writes=[krt])
    S.op('dve', lambda e: e.scalar_tensor_tensor(out=dst, in0=ps[:], scalar=gcol_ap, in1=rth[:], op0=ALU.mult, op1=ALU.mult),
         reads=[pk, krt, 'pv'], writes=[dkey])
    S.op('dve', lambda e: e.scalar_tensor_tensor(out=pv3(t1[:]), in0=pv3(ps[0:32, :]), scalar=gcol_ap[0:32, :], in1=Cv, op0=ALU.mult, op1=ALU.mult),
         reads=[pk, tb.get('Ckey', 'C'), 'pv'], writes=[k1])
    S.op('dve', lambda e: e.tensor_tensor(out=pv3(t2[:]), in0=pv3(ps3[0:32, :]), in1=Sv, op=ALU.mult), reads=[pk3, tb.get('Skey', 'S')], writes=[k2])
    S.op('pool', lambda e: e.tensor_tensor(out=t1[:], in0=t1[:], in1=t2[:], op=ALU.add), reads=[k1, k2], writes=[k1])
    S.op('pool', lambda e: e.tensor_tensor(out=dst[0:32, :], in0=t1[:], in1=rth[0:32, :], op=ALU.mult), reads=[k1, krt], writes=[dkey])


def make_hw(b, pv, nsets):
    sets = []
    for n in range(nsets):
        sets.append({'sq': b.sb("hsq%d" % n, [128, T], BF16), 'x32': b.sb("hx32_%d" % n, [32, T], BF16), 'rt': b.sb("hrt%d" % n, [128, T], F32),
                     't1': b.sb("ht1_%d" % n, [32, T], F32), 't2': b.sb("ht2_%d" % n, [32, T], F32)})
    return {'sets': sets, 'i': [0], 'eps': pv[:, PV_EPS:PV_EPS + 1]}


def make_permG(b, p0, pv, gcol, name):
    t = b.sb(name, [32, 32], BF16)
    b.S.op('dve', lambda e: e.tensor_scalar(out=t[:], in0=p0[:], scalar1=pv[0:32, gcol:gcol + 1], scalar2=None, op0=ALU.mult),
           reads=['p0', 'pv'], writes=[name])
    return t


def cast_weight(b, name, src_ap, shape, extra_reads=()):
    if name in b.wcache:
        return b.wcache[name]
    dst = b.dram_tmp("wb_" + name, shape, BF16)
    rows = shape[0]
    step = max(128, rows // 4)
    r0 = 0
    keys = []
    while r0 < rows:
        r1 = min(rows, r0 + step)
        ck = b.wcache.get('_ci', 0)
        b.wcache['_ci'] = ck + 1
        b.S.dma('pool', 'k%d' % (ck % 4), (lambda r0, r1: lambda e: e.dma_start(out=dst[r0:r1, :], in_=src_ap[r0:r1, :]))(r0, r1),
                reads=list(extra_reads), writes=[('wb', name, r0)])
        keys.append(('wb', name, r0))
        r0 = r1
    b.wcache[name] = (dst, keys)
    return dst, keys


NB = 4


def load_tables(b, dd_, i, Ct, St):
    S = b.S

    def compute(_s, _k):
        S.dma('sp', 'ct%d' % (i % 2), lambda e: e.dma_start(out=Ct[i % 2][:], in_=dd_['ctab'][:, i * T:(i + 1) * T]), writes=[('Ct', i % 2)])
        S.dma('sp', 'st%d' % (i % 2), lambda e: e.dma_start(out=St[i % 2][:], in_=dd_['stab'][:, i * T:(i + 1) * T]), writes=[('St', i % 2)])
    b.job([], compute)
    return {'C': Ct[i % 2], 'S': St[i % 2], 'Ckey': ('Ct', i % 2), 'Skey': ('St', i % 2)}


def build_stageA(mode, prog):
    nc = prog['nc']
    dd_ = prog['dram']
    es = ExitStack()
    with es:
        b = Bld(nc, es, pfx="a%d_" % mode, sem_es=prog['sem_es'], dram=prog['dram'], wcache=prog['wcache'], S=prog['S'])
        S = b.S
        xT = dd_["xT"]
        xh = dd_["xh"]
        pvd = dd_["pvec"]
        pos32 = dd_["pos32"]
        asp, usp = dd_["asp"], dd_["usp"]
        h1T = dd_["h1T"]
        KTo = [dd_["KT%d" % g] for g in range(3)]
        Vo = [dd_["V%d" % g] for g in range(3)]

        pv = b.sb("pv", [128, NPVT], F32)
        b.cload(pv[:], pvd, 'pv')
        ones = b.sb("ones", [128, 128], BF16)
        S.op('pool', lambda e: e.memset(ones[:], 1.0), writes=['ones'])
        wb_in, k_in = cast_weight(b, "a_w_in", dd_["a_w_in"], [D, 2 * D])
        xt = [b.sb("xt%d" % i, [128, KC, T], F32) for i in range(2)]
        hn = b.sb("hn", [128, KC, T], BF16)
        sq = b.sb("sq", [128, 2, T], BF16)
        rt = b.sb("rt", [128, T], F32)
        state = b.sb("state", [128, 8], F32)

        def xload(i):
            def ld(_s, _k, i=i):
                x = xt[i % 2]
                xkey = 'x%d' % (i % 2)
                S.dma('sp', xkey, lambda e: e.dma_start(out=x[:], in_=xT[:, i * T:(i + 1) * T].rearrange("(c p) n -> p c n", p=128)),
                      writes=[(xkey, c) for c in range(KC)])
            b.job([], ld)

        if mode == 1:
            emit_rope_tables_all(b, pos32, pv, dd_)
            ident = b.sb("ident", [128, 128], BF16)
            b.cload(ident[:], dd_["ident"], 'ident')
            gaw_s = b.sb("gaw_s", [128, 8, 128], BF16)
            gxw_s = b.sb("gxw_s", [128, 8, 128], BF16)
            S.dma('pool', 'gw', lambda e: e.dma_start(out=gaw_s[:], in_=dd_["gaw"]), writes=['gaw'])
            S.dma('pool', 'gw', lambda e: e.dma_start(out=gxw_s[:], in_=dd_["gxw"]), writes=['gxw'])
            dg = b.sb("dg", [128, 32, 128], F32)
            for kc_ in range(32):
                S.op('dve', (lambda kc_: lambda e: e.tensor_scalar(out=dg[:, kc_, :], in0=ident[:], scalar1=pv[:, PV_CONVW + kc_:PV_CONVW + kc_ + 1], scalar2=None, op0=ALU.mult))(kc_),
                     reads=['ident', 'pv'], writes=[('dg', kc_)])
            cl = b.sb("cl", [128, 8], F32)
            yy = b.sb("yy", [128, 8], F32)
            ser = b.sb("ser", [128, 8], F32)
            lnv = b.sb("lnv", [128, 8], F32)
            msk = b.sb("msk", [128, 8], F32)
            lam = pv[:, PV_LAM:PV_LAM + 8]
            S.op('act', lambda e: e.activation(out=yy[:], in_=lam, func=AF.Exp, scale=-1.0), reads=['pv'], writes=['yy'])
            S.op('dve', lambda e: e.tensor_scalar(out=ser[:], in0=yy[:], scalar1=-0.25, scalar2=1.0 / 3, op0=ALU.mult, op1=ALU.add), reads=['yy'], writes=['ser'])
            S.op('dve', lambda e: e.tensor_tensor(out=ser[:], in0=ser[:], in1=yy[:], op=ALU.mult), reads=['ser', 'yy'], writes=['ser'])
            S.op('dve', lambda e: e.tensor_scalar(out=ser[:], in0=ser[:], scalar1=-1.0, scalar2=0.5, op0=ALU.mult, op1=ALU.add), reads=['ser'], writes=['ser'])
            S.op('dve', lambda e: e.tensor_tensor(out=ser[:], in0=ser[:], in1=yy[:], op=ALU.mult), reads=['ser', 'yy'], writes=['ser'])
            S.op('dve', lambda e: e.tensor_scalar(out=ser[:], in0=ser[:], scalar1=-1.0, scalar2=1.0, op0=ALU.mult, op1=ALU.add), reads=['ser'], writes=['ser'])
            S.op('dve', lambda e: e.tensor_tensor(out=ser[:], in0=ser[:], in1=yy[:], op=ALU.mult), reads=['ser', 'yy'], writes=['ser'])
            S.op('act', lambda e: e.activation(out=lnv[:], in_=yy[:], func=AF.Ln, scale=1.0, bias=pv[:, PV_ONE:PV_ONE + 1]), reads=['yy', 'pv'], writes=['lnv'])
            S.op('dve', lambda e: e.tensor_scalar(out=msk[:], in0=yy[:], scalar1=0.05, scalar2=None, op0=ALU.is_lt), reads=['yy'], writes=['msk'])
            S.op('dve', lambda e: e.tensor_tensor(out=ser[:], in0=ser[:], in1=lnv[:], op=ALU.subtract), reads=['ser', 'lnv'], writes=['ser'])
            S.op('dve', lambda e: e.tensor_tensor(out=ser[:], in0=ser[:], in1=msk[:], op=ALU.mult), reads=['ser', 'msk'], writes=['ser'])
            S.op('dve', lambda e: e.tensor_tensor(out=ser[:], in0=ser[:], in1=lnv[:], op=ALU.add), reads=['ser', 'lnv'], writes=['ser'])
            S.op('dve', lambda e: e.tensor_scalar(out=cl[:], in0=ser[:], scalar1=-8.0, scalar2=None, op0=ALU.mult), reads=['ser'], writes=['cl'])

            xb = b.sb("xb", [128, KC, T + 3], F32)
            xhs = b.sb("xhs", [128, KC, 4], F32)
            hnh = b.sb("hnh", [128, KC, 4], BF16)
            xc = b.sb("xc", [128, 2, T], F32)
            xcb = b.sb("xcb", [128, 2, T], BF16)
            rg = b.sb("rg", [128, 2, T], F32)
            ig = b.sb("ig", [128, 2, T], F32)
            av = b.sb("av", [128, NB, T], F32)
            a2 = b.sb("a2", [128, 2, T], F32)
            uv = b.sb("uv", [128, NB, T], F32)
            hr = b.sb("hr", [128, 2, T], F32)
            rsum = b.sb("rsum", [128, 8, NT], F32)
            S.op('dve', lambda e: e.memset(state[:], 0.0), writes=[('state', c) for c in range(KC)])

            b.cload(xhs[:], xh.rearrange("(c p) n -> p c n", p=128), [('xh', c) for c in range(KC)])
            emit_rmsnorm(b, xhs, 'xh', 4, PV_ANORM, pv, hnh, 'hnh', ones, sq, rt, 'halo')

            def h_halo(oc, ps, pk):
                S.op('act', lambda e: e.activation(out=xb[:, oc, T:T + 3], in_=ps[:, 1:4], func=AF.Identity), reads=[pk], writes=[('xb', oc)])
            emit_linear(b, wb_in[:, D:2 * D], k_in, D, KC, lambda kc: hnh[:, kc, :], lambda kc: ('hnh', kc), h_halo, 4)

            xload(0)
            for i in range(NT):
                x = xt[i % 2]
                xkey = 'x%d' % (i % 2)
                if i + 1 < NT:
                    xload(i + 1)
                emit_rmsnorm(b, x, xkey, T, PV_ANORM, pv, hn, 'hn', ones, sq, rt, 'a')

                def h_xb(oc, ps, pk):
                    S.op('dve', lambda e: e.tensor_copy(out=xb[:, oc, 0:3], in_=xb[:, oc, T:T + 3]), reads=[('xb', oc)], writes=[('xb', oc)])
                    S.op('dve', lambda e: e.tensor_copy(out=xb[:, oc, 3:T + 3], in_=ps[:]), reads=[pk], writes=[('xb', oc)])
                emit_linear(b, wb_in[:, D:2 * D], k_in, D, KC, lambda kc: hn[:, kc, :], lambda kc: ('hn', kc), h_xb, T)

                def lru(_s, _k, i=i):
                    def steps_for(c):
                        j = c % NB
                        q = c % 2
                        st_ = {}
                        sl = []

                        def s_conv():
                            st_['psc'], st_['pkc'] = b.bank()
                            psc, pkc = st_['psc'], st_['pkc']
                            for k in range(4):
                                S.op('pe', (lambda k: lambda e: e.matmul(psc[:], lhsT=dg[:, 8 * k + c, :], rhs=xb[:, c, k:k + T], start=(k == 0), stop=(k == 3)))(k),
                                     reads=[('dg', 8 * k + c), ('xb', c)], writes=[pkc])
                        sl.append(s_conv)

                        def s_xc():
                            psc, pkc = st_['psc'], st_['pkc']
                            S.op('act', lambda e: e.activation(out=xc[:, q, :], in_=psc[:], func=AF.Identity, bias=pv[:, PV_CONVB + c:PV_CONVB + c + 1]),
                                 reads=[pkc, 'pv'], writes=[('xc', q)])
                            S.op('dve', lambda e: e.tensor_scalar(out=xcb[:, q, :], in0=psc[:], scalar1=pv[:, PV_CONVB + c:PV_CONVB + c + 1], scalar2=None, op0=ALU.add),
                                 reads=[pkc, 'pv'], writes=[('xcb', q)])
                        sl.append(s_xc)

                        def s_gates():
                            st_['ps1'], st_['pk1'] = b.bank()
                            st_['ps2'], st_['pk2'] = b.bank()
                            ps1, ps2 = st_['ps1'], st_['ps2']
                            S.op('pe', lambda e: e.matmul(ps1[:], lhsT=gaw_s[:, c, :], rhs=xcb[:, q, :], start=True, stop=True), reads=['gaw', ('xcb', q)], writes=[st_['pk1']])
                            S.op('pe', lambda e: e.matmul(ps2[:], lhsT=gxw_s[:, c, :], rhs=xcb[:, q, :], start=True, stop=True), reads=['gxw', ('xcb', q)], writes=[st_['pk2']])
                        sl.append(s_gates)

                        def s_sig():
                            ps1, ps2 = st_['ps1'], st_['ps2']
                            S.op('act', lambda e: e.activation(out=rg[:, q, :], in_=ps1[:], func=AF.Sigmoid, bias=pv[:, PV_GAB + c:PV_GAB + c + 1], accum_out=rsum[:, c, i:i + 1]),
                                 reads=[st_['pk1'], 'pv'], writes=[('rg', q), ('rsum', c, i)])
                            S.op('act', lambda e: e.activation(out=ig[:, q, :], in_=ps2[:], func=AF.Sigmoid, bias=pv[:, PV_GXB + c:PV_GXB + c + 1]),
                                 reads=[st_['pk2'], 'pv'], writes=[('ig', q)])
                        sl.append(s_sig)

                        def s_exp():
                            S.op('act', lambda e: e.activation(out=av[:, j, :], in_=rg[:, q, :], func=AF.Exp, scale=cl[:, c:c + 1]), reads=[('rg', q), 'cl'], writes=[('av', j)])
                            S.op('dve', lambda e: e.tensor_tensor(out=uv[:, j, :], in0=ig[:, q, :], in1=xc[:, q, :], op=ALU.mult), reads=[('ig', q), ('xc', q)], writes=[('uv', j)])
                        sl.append(s_exp)

                        def s_sq():
                            S.op('dve', lambda e: e.tensor_tensor(out=a2[:, q, :], in0=av[:, j, :], in1=av[:, j, :], op=ALU.mult), reads=[('av', j)], writes=[('a2', q)])
                        sl.append(s_sq)

                        def s_sqrt():
                            S.op('act', lambda e: e.activation(out=a2[:, q, :], in_=a2[:, q, :], func=AF.Sqrt, scale=-1.0, bias=pv[:, PV_ONE:PV_ONE + 1]), reads=[('a2', q), 'pv'], writes=[('a2', q)])
                        sl.append(s_sqrt)

                        def s_u():
                            S.op('dve', lambda e: e.tensor_tensor(out=uv[:, j, :], in0=uv[:, j, :], in1=a2[:, q, :], op=ALU.mult), reads=[('uv', j), ('a2', q)], writes=[('uv', j)])
                            S.dma('sp', 'sa%d' % j, lambda e: e.dma_start(out=asp[:, i, c, :], in_=av[:, j, :]), reads=[('av', j)])
                            S.dma('sp', 'su%d' % j, lambda e: e.dma_start(out=usp[:, i, c, :], in_=uv[:, j, :]), reads=[('uv', j)])
                        sl.append(s_u)

                        def s_scan():
                            S.op('dve', lambda e: e.tensor_tensor_scan(out=hr[:, c % 2, :], data0=av[:, j, :], data1=uv[:, j, :], initial=state[:, c:c + 1], op0=ALU.mult, op1=ALU.add),
                                 reads=[('av', j), ('uv', j), ('state', c)], writes=[('hr', c % 2)])
                        sl.append(s_scan)

                        def s_state():
                            S.op('dve', lambda e: e.tensor_copy(out=state[:, c:c + 1], in_=hr[:, c % 2, T - 1:T]), reads=[('hr', c % 2)], writes=[('state', c)])
                        sl.append(s_state)
                        return sl

                    for c0 in range(0, KC, 2):
                        sa_, sb_ = steps_for(c0), steps_for(c0 + 1)
                        for fa, fb in zip(sa_, sb_):
                            fa()
                            fb()
                b.job([], lru)
                if i == 0:
                    def latecast(_s, _k):
                        er = [('av', 0)]
                        cast_weight(b, "a_w_out", dd_["a_w_out"], [D, D], er)
                        cast_weight(b, "a_ffn_w_in", dd_["a_ffn_w_in"], [D, 2 * FFN], er)
                        cast_weight(b, "a_ffn_w_out", dd_["a_ffn_w_out"], [FFN, D], er)
                        cast_weight(b, "w_kv", dd_["w_kv"], [D, 1536], er)
                    b.job([], latecast)

            def fin(_s, _k):
                rs = b.sb("rs", [128, 8], F32)
                ab = b.sb("ab", [128, 16], F32)
                S.op('dve', lambda e: e.reduce_sum(out=rs[:], in_=rsum[:], axis=AX.X), reads=[('rsum', c, i) for c in range(8) for i in range(NT)], writes=['rs'])
                S.op('dve', lambda e: e.tensor_tensor(out=rs[:], in0=rs[:], in1=cl[:], op=ALU.mult), reads=['rs', 'cl'], writes=['rs'])
                S.op('act', lambda e: e.activation(out=ab[:, 0:8], in_=rs[:], func=AF.Exp), reads=['rs'], writes=['ab'])
                S.op('dve', lambda e: e.tensor_copy(out=ab[:, 8:16], in_=state[:]), reads=[('state', c) for c in range(KC)] + ['ab'], writes=['ab'])
                S.dma('sp', 'abo', lambda e: e.dma_start(out=dd_['AB'], in_=ab[:]), reads=['ab'])
            b.job([], fin)
            b.run_jobs()
            cast_weight(b, "b_w_q", dd_["b_w_q"], [D, 3072])
            cast_weight(b, "b_w_o", dd_["b_w_o"], [D, D])
            cast_weight(b, "b_ffn_w_in", dd_["b_ffn_w_in"], [D, 2 * FFN])
            cast_weight(b, "b_ffn_w_out", dd_["b_ffn_w_out"], [FFN, D])
            S.barrier_all(skip=('k0', 'k1', 'k2', 'k3'))
            S.dma('pool', 'cc', lambda e: e.collective_compute("AllGather", op=ALU.bypass, replica_groups=[[0, 1, 2, 3], [4, 5, 6, 7]],
                                                               ins=[dd_['AB'].opt()], outs=[dd_['ABg'].opt()]), inc=1)
            S.barrier_all(skip=('k0', 'k1', 'k2', 'k3'))
            return

        wb_out, k_out = cast_weight(b, "a_w_out", None, None)
        wb_fin, k_fin = cast_weight(b, "a_ffn_w_in", None, None)
        wb_fout, k_fout = cast_weight(b, "a_ffn_w_out", None, None)
        wb_kv, k_kv = cast_weight(b, "w_kv", None, None)
        av8 = b.sb("av8", [128, KC, T], F32)
        uv8 = b.sb("uv8", [128, KC, T], F32)
        hr = b.sb("hr", [128, 2, T], F32)
        gate = b.sb("gate", [128, KC, T], BF16)
        gh = b.sb("gh", [128, KC, T], BF16)
        sg = b.sb("sg", [128, 2, T], BF16)
        mt = b.sb("mt", [128, FC, T], BF16)
        aball = b.sb("aball", [128, 4, 16], F32)
        ae = b.sb("ae", [128, 8], F32)
        be = b.sb("be", [128, 8], F32)
        p0 = b.sb("p0s", [32, 32], F32)
        b.cload(p0[:], dd_["p0"], 'p0')
        b.cload(aball[:], dd_["ABall"], 'aball')
        Ct = [b.sb("Ct%d" % i, [32, T], F32) for i in range(2)]
        St = [b.sb("St%d" % i, [32, T], F32) for i in range(2)]
        hw = make_hw(b, pv, 2)
        permK = [make_permG(b, p0, pv, PV_KN + g, "permK%d" % g) for g in range(3)]
        kts = b.sb("kts", [128, 2, T], BF16)
        vts = b.sb("vts", [128, 2, 256], BF16)
        vbi = [0]
        voi = [0]

        S.op('dve', lambda e: e.memset(state[:], 0.0), writes=['state'])
        for i in range(4):
            A_i = aball[:, i, 0:8]
            B_i = aball[:, i, 8:16]
            sel = pv[:, PV_SEL + i:PV_SEL + i + 1]
            S.op('dve', (lambda A_i, sel: lambda e: e.tensor_scalar(out=ae[:], in0=A_i, scalar1=-1.0, scalar2=sel, op0=ALU.add, op1=ALU.mult))(A_i, sel),
                 reads=['aball', 'pv'], writes=['ae'])
            S.op('dve', lambda e: e.tensor_scalar(out=ae[:], in0=ae[:], scalar1=1.0, scalar2=None, op0=ALU.add), reads=['ae'], writes=['ae'])
            S.op('dve', (lambda B_i, sel: lambda e: e.tensor_scalar(out=be[:], in0=B_i, scalar1=sel, scalar2=None, op0=ALU.mult))(B_i, sel),
                 reads=['aball', 'pv'], writes=['be'])
            S.op('dve', lambda e: e.tensor_tensor(out=state[:], in0=state[:], in1=ae[:], op=ALU.mult), reads=['state', 'ae'], writes=['state'])
            S.op('dve', lambda e: e.tensor_tensor(out=state[:], in0=state[:], in1=be[:], op=ALU.add), reads=['state', 'be'], writes=['state'])

        xload(0)
        for i in range(NT):
            x = xt[i % 2]
            xkey = 'x%d' % (i % 2)
            if i + 1 < NT:
                xload(i + 1)

            def aul(_s, _k, i=i):
                S.dma('sp', 'la', lambda e: e.dma_start(out=av8[:], in_=asp[:, i, :, :]), writes=[('av8', c) for c in range(KC)])
                S.dma('sp', 'lu', lambda e: e.dma_start(out=uv8[:], in_=usp[:, i, :, :]), writes=[('uv8', c) for c in range(KC)])
            b.job([], aul)
            tbl = load_tables(b, dd_, i, Ct, St)
            emit_rmsnorm(b, x, xkey, T, PV_ANORM, pv, hn, 'hn', ones, sq, rt, 'a')

            def h_gate(oc, ps, pk):
                S.op('act', lambda e: e.activation(out=gate[:, oc, :], in_=ps[:], func=AF.Gelu_apprx_tanh), reads=[pk], writes=[('gate', oc)])
            emit_linear(b, wb_in[:, 0:D], k_in, D, KC, lambda kc: hn[:, kc, :], lambda kc: ('hn', kc), h_gate, T)

            def lru2(_s, _k, i=i):
                for c in range(KC):
                    j = c % 2
                    S.op('dve', (lambda c, j: lambda e: e.tensor_tensor_scan(out=hr[:, j, :], data0=av8[:, c, :], data1=uv8[:, c, :], initial=state[:, c:c + 1],
                                                                              op0=ALU.mult, op1=ALU.add))(c, j),
                         reads=[('av8', c), ('uv8', c), 'state'], writes=[('hr', j)])
                    S.op('dve', (lambda c, j: lambda e: e.tensor_copy(out=state[:, c:c + 1], in_=hr[:, j, T - 1:T]))(c, j), reads=[('hr', j)], writes=['state'])
                    S.op('dve', (lambda c, j: lambda e: e.tensor_tensor(out=gh[:, c, :], in0=gate[:, c, :], in1=hr[:, j, :], op=ALU.mult))(c, j),
                         reads=[('gate', c), ('hr', j)], writes=[('gh', c)])
            b.job([], lru2)

            def h_wo(oc, ps, pk, x=x, xkey=xkey):
                S.op('dve', lambda e: e.tensor_tensor(out=x[:, oc, :], in0=ps[:], in1=x[:, oc, :], op=ALU.add), reads=[pk, (xkey, oc)], writes=[(xkey, oc)])
            emit_linear(b, wb_out, k_out, D, KC, lambda kc: gh[:, kc, :], lambda kc: ('gh', kc), h_wo, T)
            emit_ffn(b, x, xkey, PV_AFFN, pv, hn, ones, sq, rt, wb_fin, k_fin, wb_fout, k_fout, sg, mt)

            def st(_s, _k, i=i, x=x, xkey=xkey):
                S.dma('pool', 'h1o', lambda e: e.dma_start(out=h1T[:, i * T:(i + 1) * T].rearrange("(c p) n -> p c n", p=128), in_=x[:]),
                      reads=[(xkey, c) for c in range(KC)])
            b.job([], st)
            emit_rmsnorm(b, x, xkey, T, PV_KVN, pv, hn, 'hn', ones, sq, rt, 'kv')
            for g in range(3):
                d = DIL[g]

                def h_k(oc, ps, pk, g=g, d=d, i=i, tbl=tbl):
                    hk = oc
                    emit_head_post(b, ps, pk, pv[:, PV_KN + g:PV_KN + g + 1], permK[g][:], "permK%d" % g, tbl, d, kts[:, hk, :], ('kts', hk), ones, hw)
                    if hk == 1:
                        if g < 2:
                            dst = KTo[g][:, :, 4 * i:4 * i + 4, :].rearrange("p h b k -> p h (b k)")
                            S.dma('pool', 'kto', (lambda dst: lambda e: e.dma_start(out=dst, in_=kts[:]))(dst), reads=[('kts', 0), ('kts', 1)])
                        else:
                            s_, j_ = i // 4, i % 4
                            for h2 in range(2):
                                dst = KTo[2][:, h2, 16 * s_:16 * s_ + 16, 32 * j_:32 * j_ + 32]
                                S.dma('pool', 'kto', (lambda dst, h2: lambda e: e.dma_start(out=dst, in_=kts[:, h2, :].rearrange("p (ph l) -> p ph l", ph=16)))(dst, h2),
                                      reads=[('kts', 0), ('kts', 1)])
                rhs_fn = (lambda d: (lambda kc: perm_view(hn[:, kc, :], d)))(d)
                emit_linear_perm(b, wb_kv[:, g * 512:g * 512 + 256], k_kv, 256, KC, rhs_fn, lambda kc: ('hn', kc), h_k, T, d)

                def vloads(g=g):
                    def lf(slab):
                        dst = slab[:, 0:KC * 256].rearrange("p (k n) -> p k n", k=KC)
                        return dst, wb_kv.rearrange("(k p) n -> p k n", p=128)[:, :, g * 512 + 256:g * 512 + 512], k_kv
                    return [lf]

                def vcomp(slab, skey, g=g, i=i):
                    sv = slab[:, 0:KC * 256].rearrange("p (k n) -> p k n", k=KC)
                    for blk in range(4):
                        vb = vbi[0] % 2
                        vbi[0] += 1
                        psa, pka = b.bank()
                        for kc in range(KC):
                            S.op('pe', (lambda kc, blk, psa: lambda e: e.matmul(psa[:, 0:256], lhsT=hn[:, kc, blk * 128:(blk + 1) * 128], rhs=sv[:, kc, :],
                                                                               start=(kc == 0), stop=(kc == KC - 1)))(kc, blk, psa),
                                 reads=[skey, ('hn', kc)], writes=[pka])
                        S.op('act', (lambda vb, psa: lambda e: e.activation(out=vts[:, vb, :], in_=psa[:, 0:256], func=AF.Identity))(vb, psa), reads=[pka], writes=[('vts', vb)])

                        def vdma(dst, src, vb=vb):
                            k = 'vo%d' % (voi[0] % 8)
                            voi[0] += 1
                            S.dma('pool', k, lambda e: e.dma_start(out=dst, in_=src), reads=[('vts', vb)])
                        if g == 0:
                            r0 = (4 * i + blk) * 128
                            vdma(Vo[0][r0:r0 + 128, :], vts[:, vb, :])
                        elif g == 1:
                            for p in range(4):
                                r0 = (4 * i + p) * 128 + 32 * blk
                                vdma(Vo[1][r0:r0 + 32, :], vts[p::4, vb, :])
                        else:
                            for p in range(16):
                                r0 = (16 * (i // 4) + p) * 128 + 32 * (i % 4) + 8 * blk
                                vdma(Vo[2][r0:r0 + 8, :], vts[p::16, vb, :])
                b.job(vloads(), vcomp)

        b.run_jobs()
        S.barrier_all()
        for ti, segs in enumerate(TAILS):
            tb_ = dd_['tail_b%d' % ti]
            for (kind, g, off, n) in segs:
                d = DIL[g]
                if kind == 'K':
                    src = dd_['KTh%d' % g][:, :, 32:32 + d, :]
                    dst = tb_[:, off:off + n].rearrange("p (h b k) -> p h b k", h=2, b=d)
                else:
                    src = dd_['Vh%d' % g][32 * 128:(32 + d) * 128, :].rearrange("(b k) n -> k b n", k=128)
                    dst = tb_[:, off:off + n].rearrange("p (b n) -> p b n", b=d)
                S.dma('sp', 'pk%d' % ti, (lambda dst, src: lambda e: e.dma_start(out=dst, in_=src))(dst, src))
        S.barrier_all()
        for ti in range(len(TAILS)):
            S.dma('pool', 'cc', (lambda ti: lambda e: e.collective_compute("AllGather", op=ALU.bypass, replica_groups=[[0, 1, 2, 3], [4, 5, 6, 7]],
                                                                           ins=[dd_['tail_b%d' % ti].opt()], outs=[dd_['tail_g%d' % ti].opt()]))(ti), inc=1)
        S.barrier_all()


def build_stageB(prog):
    nc = prog['nc']
    dd_ = prog['dram']
    es = ExitStack()
    with es:
        b = Bld(nc, es, pfx="b_", sem_es=prog['sem_es'], dram=prog['dram'], wcache=prog['wcache'], S=prog['S'])
        S = b.S
        h1T = dd_["h1T"]
        pvd = dd_["pvec"]
        p0d = dd_["p0"]
        mown_d = dd_["mown"]
        mprev_d = dd_["mprev"]
        KTh = [dd_["KTh%d" % g] for g in range(3)]
        Vh = [dd_["Vh%d" % g] for g in range(3)]
        outT = dd_["outT"]

        pv = b.sb("pv", [128, NPVT], F32)
        b.cload(pv[:], pvd, 'pv')
        ones = b.sb("ones", [128, 128], BF16)
        S.op('pool', lambda e: e.memset(ones[:], 1.0), writes=['ones'])
        p0 = b.sb("p0s", [32, 32], F32)
        b.cload(p0[:], p0d, 'p0')
        mown = b.sb("mown_s", [128, 4, 128], BF16)
        mprev = b.sb("mprev_s", [128, 4, 128], BF16)
        b.cload(mown[:], mown_d, 'mown')
        b.cload(mprev[:], mprev_d, 'mprev')
        wb_q, k_q = cast_weight(b, "b_w_q", None, None)
        wb_o, k_o = cast_weight(b, "b_w_o", None, None)
        wb_fin, k_fin = cast_weight(b, "b_ffn_w_in", None, None)
        wb_fout, k_fout = cast_weight(b, "b_ffn_w_out", None, None)
        ident = b.sb("ident", [128, 128], BF16)
        b.cload(ident[:], dd_["ident"], 'ident')

        if True:
            for ti, segs in enumerate(TAILS):
                tg_ = dd_['tail_g%d' % ti]
                tsel = dd_['tail_s%d' % ti]
                CH = TAILW[ti]
                acc = b.slabs[0]
                for r in range(4):
                    cand = b.slabs[1 + (r % 2)]
                    ck = ('slab', 1 + (r % 2))
                    S.dma('sp', 'hs%d' % (r % 2), (lambda cand, r, tg_, CH: lambda e: e.dma_start(out=cand[:, 0:CH], in_=tg_[r * 128:(r + 1) * 128, :]))(cand, r, tg_, CH),
                          writes=[ck])
                    selc = pv[:, PV_SEL + 4 + r:PV_SEL + 5 + r]
                    if r == 0:
                        S.op('dve', (lambda cand, selc, CH: lambda e: e.tensor_scalar(out=acc[:, 0:CH], in0=cand[:, 0:CH], scalar1=selc, scalar2=None, op0=ALU.mult))(cand, selc, CH),
                             reads=[ck, 'pv'], writes=[('slab', 0)])
                    else:
                        S.op('dve', (lambda cand, selc, CH: lambda e: e.scalar_tensor_tensor(out=acc[:, 0:CH], in0=cand[:, 0:CH], scalar=selc, in1=acc[:, 0:CH], op0=ALU.mult, op1=ALU.add))(cand, selc, CH),
                             reads=[ck, 'pv', ('slab', 0)], writes=[('slab', 0)])
                S.dma('sp', 'hso', (lambda tsel, CH: lambda e: e.dma_start(out=tsel, in_=acc[:, 0:CH]))(tsel, CH), reads=[('slab', 0)])
            S.barrier_all()
            for ti, segs in enumerate(TAILS):
                tsel = dd_['tail_s%d' % ti]
                for (kind, g, off, n) in segs:
                    d = DIL[g]
                    if kind == 'K':
                        dst = dd_['KTh%d' % g][:, :, 0:d, :]
                        src = tsel[:, off:off + n].rearrange("p (h b k) -> p h b k", h=2, b=d)
                    else:
                        dst = dd_['Vh%d' % g][0:d * 128, :].rearrange("(b k) n -> k b n", k=128)
                        src = tsel[:, off:off + n].rearrange("p (b n) -> p b n", b=d)
                    S.dma('sp', 'uk%d' % ti, (lambda dst, src: lambda e: e.dma_start(out=dst, in_=src))(dst, src))
            S.barrier_all()
        xt = [b.sb("xt%d" % i, [128, KC, T], F32) for i in range(2)]
        hn = b.sb("hn", [128, KC, T], BF16)
        sq = b.sb("sq", [128, 2, T], BF16)
        rt = b.sb("rt", [128, T], F32)
        sg = b.sb("sg", [128, 2, T], BF16)
        mt = b.sb("mt", [128, FC, T], BF16)
        Ct = [b.sb("Ct%d" % i, [32, T], F32) for i in range(2)]
        St = [b.sb("St%d" % i, [32, T], F32) for i in range(2)]
        hw = make_hw(b, pv, 3)
        permQ = [make_permG(b, p0, pv, PV_QN + g, "permQ%d" % g) for g in range(3)]
        QT = b.sb("QT", [128, 4, T], BF16)
        NUM = b.sb("NUM", [128, 4, T], F32)
        DEN = b.sb("DEN", [128, 4, T], F32)
        OT = b.sb("OT", [128, 8, T], BF16)
        KTb = [b.sb("KTs0", [128, 8, 128], BF16), b.sb("KTs1", [128, 8, 128], BF16), b.sb("KTs2", [128, 32, 128], BF16)]
        Vb = [b.sb("Vs0", [128, 8, 128], BF16), b.sb("Vs1", [128, 8, 128], BF16), b.sb("Vs2", [128, 32, 128], BF16)]
        Pt = [b.sb("Pt%d" % i, [128, 512], BF16) for i in range(4)]
        pti = [0]

        def xload(i):
            def ld(_s, _k, i=i):
                x = xt[i % 2]
                xkey = 'x%d' % (i % 2)
                S.dma('sp', xkey, lambda e: e.dma_start(out=x[:], in_=h1T[:, i * T:(i + 1) * T].rearrange("(c p) n -> p c n", p=128)),
                      writes=[(xkey, c) for c in range(KC)])
            b.job([], ld)

        xload(0)
        for i in range(NT):
            x = xt[i % 2]
            xkey = 'x%d' % (i % 2)
            if i + 1 < NT:
                xload(i + 1)
            tb = load_tables(b, dd_, i, Ct, St)
            emit_rmsnorm(b, x, xkey, T, PV_BN, pv, hn, 'hn', ones, sq, rt, 'b')

            for hk in range(2):
                def kvld(_s, _k, i=i, hk=hk):
                    for g in range(3):
                        if g == 0:
                            b0, nb = 4 * i, 5
                        elif g == 1:
                            b0, nb = 4 * i, 8
                        else:
                            b0, nb = 16 * (i // 4), 32
                        S.dma('sp', 'ktl%d' % g, (lambda g, b0, nb: lambda e: e.dma_start(out=KTb[g][:, 0:nb, :], in_=KTh[g][:, hk, b0:b0 + nb, :]))(g, b0, nb), writes=[('KTs', g)])
                        S.dma('sp', 'vl%d' % g, (lambda g, b0, nb: lambda e: e.dma_start(out=Vb[g][:, 0:nb, :], in_=Vh[g][b0 * 128:(b0 + nb) * 128, hk * 128:(hk + 1) * 128].rearrange("(b k) n -> k b n", k=128)))(g, b0, nb),
                              writes=[('Vs', g)])
                b.job([], kvld)
                for g in range(3):
                    d = DIL[g]
                    nq = 128 if g < 2 else 32
                    nqb = T // nq

                    KTs = KTb[g]
                    Vs = Vb[g]
                    kkey = ('KTs', g)
                    vkey = ('Vs', g)

                    def h_q(oc, ps, pk, g=g, d=d, tb=tb):
                        emit_head_post(b, ps, pk, pv[:, PV_QN + g:PV_QN + g + 1], permQ[g][:], "permQ%d" % g, tb, d, QT[:, oc, :], ('QT', oc), ones, hw)
                    rhs_fn = (lambda d: (lambda kc: perm_view(hn[:, kc, :], d)))(d)
                    c0 = g * 1024 + hk * 512
                    emit_linear_perm(b, wb_q[:, c0:c0 + 512], k_q, 512, KC, rhs_fn, lambda kc: ('hn', kc), h_q, T, d)

                    def attn(_s, _k, g=g, d=d, nq=nq, nqb=nqb, i=i, KTs=KTs, Vs=Vs, kkey=kkey, vkey=vkey):
                        def stA(qb):
                            if g == 0:
                                own, prev = qb + 1, qb
                                halo = (i == 0 and qb == 0)
                                lq0 = 0
                            elif g == 1:
                                own, prev = 4 + qb, qb
                                halo = (i == 0)
                                lq0 = 0
                            else:
                                own, prev = 16 + qb, qb
                                halo = (i < 4)
                                lq0 = 32 * (i % 4)
                            n4 = 4 * nq
                            rhs = QT[:, :, qb * nq:(qb + 1) * nq]
                            pts = []
                            for which, blk in (('prev', prev), ('own', own)):
                                ps, pk = b.bank()
                                mk = (mprev if which == 'prev' else mown)
                                mview = mk[:, :, lq0:lq0 + nq]
                                S.op('pe', (lambda ps, blk, rhs: lambda e: e.matmul(ps[:, 0:n4].rearrange("p (h q) -> p h q", h=4), lhsT=KTs[:, blk, :], rhs=rhs, start=True, stop=False))(ps, blk, rhs),
                                     reads=[kkey] + [('QT', hh) for hh in range(4)], writes=[pk])
                                S.op('pe', (lambda ps, mview: lambda e: e.matmul(ps[:, 0:n4].rearrange("p (h q) -> p h q", h=4), lhsT=ident[:], rhs=mview, start=False, stop=True))(ps, mview),
                                     reads=['ident', 'mown', 'mprev'], writes=[pk])
                                pt = Pt[pti[0] % 4]
                                ptk = ('Pt', pti[0] % 4)
                                pti[0] += 1
                                bias = pv[:, PV_HB:PV_HB + 1] if (which == 'prev' and halo) else pv[:, PV_ZERO:PV_ZERO + 1]
                                S.op('act', (lambda ps, pt, bias: lambda e: e.activation(out=pt[:, 0:n4], in_=ps[:, 0:n4], func=AF.Exp, scale=SCALE, bias=bias))(ps, pt, bias),
                                     reads=[pk, 'pv'], writes=[ptk])
                                pts.append((pt, ptk, blk))
                            return (qb, n4, pts)

                        def stB(st_):
                            qb, n4, pts = st_
                            psn, pkn = b.bank()
                            psd, pkd = b.bank()
                            for j, (pt, ptk, blk) in enumerate(pts):
                                S.op('pe', (lambda pt, blk, j, psn: lambda e: e.matmul(psn[:, 0:n4], lhsT=Vs[:, blk, :], rhs=pt[:, 0:n4], start=(j == 0), stop=(j == 1)))(pt, blk, j, psn),
                                     reads=[vkey, ptk], writes=[pkn])
                            for j, (pt, ptk, blk) in enumerate(pts):
                                S.op('pe', (lambda pt, j, psd: lambda e: e.matmul(psd[:, 0:n4], lhsT=ones[:], rhs=pt[:, 0:n4], start=(j == 0), stop=(j == 1)))(pt, j, psd),
                                     reads=['ones', ptk], writes=[pkd])
                            if g == 0:
                                nview = NUM[:, :, qb * 128:(qb + 1) * 128]
                                dview = DEN[:, :, qb * 128:(qb + 1) * 128]
                            else:
                                nview = NUM[:].rearrange("p h (l ph) -> p h ph l", ph=d)[:, :, qb, :]
                                dview = DEN[:].rearrange("p h (l ph) -> p h ph l", ph=d)[:, :, qb, :]
                            nkeys = [('NUM', hh) for hh in range(4)]
                            dkeys = [('DEN', hh) for hh in range(4)]
                            pn3 = psn[:, 0:n4].rearrange("p (h q) -> p h q", h=4)
                            pd3 = psd[:, 0:n4].rearrange("p (h q) -> p h q", h=4)
                            if g == 0:
                                S.op('act', (lambda pn3, nview: lambda e: e.activation(out=nview, in_=pn3, func=AF.Identity))(pn3, nview), reads=[pkn], writes=nkeys)
                                S.op('dve', (lambda pd3, dview: lambda e: e.tensor_copy(out=dview, in_=pd3))(pd3, dview), reads=[pkd], writes=dkeys)
                            else:
                                S.op('dve', (lambda pn3, nview: lambda e: e.tensor_tensor(out=nview, in0=pn3, in1=nview, op=ALU.add))(pn3, nview), reads=[pkn] + nkeys, writes=nkeys)
                                S.op('dve', (lambda pd3, dview: lambda e: e.tensor_tensor(out=dview, in0=pd3, in1=dview, op=ALU.add))(pd3, dview), reads=[pkd] + dkeys, writes=dkeys)

                        prev_st = stA(0)
                        for qb in range(1, nqb):
                            nxt = stA(qb)
                            stB(prev_st)
                            prev_st = nxt
                        stB(prev_st)
                    b.job([], attn)

                def fin_attn(_s, _k, hk=hk):
                    allk = [('DEN', h) for h in range(4)]
                    S.op('dve', lambda e: e.reciprocal(out=DEN[:], in_=DEN[:]), reads=allk, writes=allk)
                    for h in range(4):
                        S.op('dve', (lambda h: lambda e: e.tensor_tensor(out=OT[:, 4 * hk + h, :], in0=NUM[:, h, :], in1=DEN[:, h, :], op=ALU.mult))(h),
                             reads=[('NUM', h), ('DEN', h)], writes=[('OT', 4 * hk + h)])
                b.job([], fin_attn)

            def h_wo(oc, ps, pk, x=x, xkey=xkey):
                S.op('dve', lambda e: e.tensor_tensor(out=x[:, oc, :], in0=ps[:], in1=x[:, oc, :], op=ALU.add), reads=[pk, (xkey, oc)], writes=[(xkey, oc)])
            emit_linear(b, wb_o, k_o, D, KC, lambda kc: OT[:, kc, :], lambda kc: ('OT', kc), h_wo, T)
            emit_ffn(b, x, xkey, PV_BFFN, pv, hn, ones, sq, rt, wb_fin, k_fin, wb_fout, k_fout, sg, mt)

            def st(_s, _k, i=i, x=x, xkey=xkey):
                S.dma('pool', 'oo', lambda e: e.dma_start(out=outT[:, i * T:(i + 1) * T].rearrange("(c p) n -> p c n", p=128), in_=x[:]),
                      reads=[(xkey, c) for c in range(KC)])
            b.job([], st)

        b.run_jobs()
        S.barrier_all()


TAILS = [[('K', 0, 0, 256), ('K', 1, 256, 1024), ('V', 0, 1280, 256), ('V', 1, 1536, 1024)],
         [('K', 2, 0, 4096)],
         [('V', 2, 0, 4096)]]
TAILW = [2560, 4096, 4096]


def build_fused():
    nc = bass.Bass("TRN2", target_bir_lowering=False)
    sem_es = ExitStack()
    with sem_es:
        dram = {}

        def ein(name, shape, dt):
            dram[name] = nc.dram_tensor(name, list(shape), dt, kind="ExternalInput").ap()

        ein("xT", [D, S_CORE], F32)
        ein("xh", [D, 4], F32)
        ein("pvec", [128, NPVT], F32)
        ein("a_w_in", [D, 2 * D], F32)
        ein("gaw", [128, 8, 128], F32)
        ein("gxw", [128, 8, 128], F32)
        ein("a_w_out", [D, D], F32)
        ein("a_ffn_w_in", [D, 2 * FFN], F32)
        ein("a_ffn_w_out", [FFN, D], F32)
        ein("w_kv", [D, 1536], F32)
        ein("pos32", [32, S_CORE], I32)
        ein("p0", [32, 32], F32)
        ein("b_w_q", [D, 3072], F32)
        ein("b_w_o", [D, D], F32)
        ein("b_ffn_w_in", [D, 2 * FFN], F32)
        ein("b_ffn_w_out", [FFN, D], F32)
        ein("ident", [128, 128], BF16)
        ein("mown", [128, 4, 128], BF16)
        ein("mprev", [128, 4, 128], BF16)
        dram["outT"] = nc.dram_tensor("outT", [D, S_CORE], F32, kind="ExternalOutput").ap()
        dram["AB"] = nc.dram_tensor("ab_bounce", [128, 16], F32).ap()
        dram["ABg"] = nc.dram_tensor("ab_gath", [4 * 128, 16], F32).ap()
        dram["ABall"] = dram["ABg"].rearrange("(r p) n -> p r n", p=128)
        dram["h1T"] = nc.dram_tensor("h1_spill", [D, S_CORE], F32).ap()
        dram["asp"] = nc.dram_tensor("a_spill", [128, NT, KC, T], F32).ap()
        dram["usp"] = nc.dram_tensor("u_spill", [128, NT, KC, T], F32).ap()
        dram["ctab"] = nc.dram_tensor("ctab", [32, S_CORE], F32).ap()
        dram["stab"] = nc.dram_tensor("stab", [32, S_CORE], F32).ap()
        for g in range(3):
            d = DIL[g]
            dram["KTh%d" % g] = nc.dram_tensor("kth%d" % g, [128, 2, 32 + d, 128], BF16).ap()
            dram["Vh%d" % g] = nc.dram_tensor("vh%d" % g, [(32 + d) * 128, 256], BF16).ap()
            dram["KT%d" % g] = dram["KTh%d" % g][:, :, d:d + 32, :]
            dram["V%d" % g] = dram["Vh%d" % g][d * 128:(d + 32) * 128, :]
        for ti in range(len(TAILS)):
            dram["tail_b%d" % ti] = nc.dram_tensor("tail_b%d" % ti, [128, TAILW[ti]], BF16).ap()
            dram["tail_g%d" % ti] = nc.dram_tensor("tail_g%d" % ti, [4 * 128, TAILW[ti]], BF16).ap()
            dram["tail_s%d" % ti] = nc.dram_tensor("tail_s%d" % ti, [128, TAILW[ti]], BF16).ap()
        S = Sched(nc, sem_es, "")
        prog = {'nc': nc, 'sem_es': sem_es, 'dram': dram, 'wcache': {}, 'S': S}
        stages = os.environ.get("STAGES", "123")
        build_stageA(1, prog)
        if "2" in stages:
            build_stageA(2, prog)
        if "3" in stages:
            build_stageB(prog)
        S.emit()
    return nc


def fm(v):
    return np.ascontiguousarray(np.asarray(v, np.float32).reshape(8, 128).T)


def make_pvec(inp, core):
    pvn = np.zeros((128, NPVT), np.float32)
    pvn[:, PV_ANORM:PV_ANORM + 8] = fm(inp["a_norm"][0])
    for k in range(4):
        pvn[:, PV_CONVW + 8 * k:PV_CONVW + 8 * k + 8] = fm(inp["a_conv_w"][0, k])
    pvn[:, PV_CONVB:PV_CONVB + 8] = fm(inp["a_conv_b"][0])
    pvn[:, PV_GAB:PV_GAB + 8] = fm(inp["a_gate_a_b"][0])
    pvn[:, PV_GXB:PV_GXB + 8] = fm(inp["a_gate_x_b"][0])
    pvn[:, PV_LAM:PV_LAM + 8] = fm(inp["a_lambda"][0])
    pvn[:, PV_AFFN:PV_AFFN + 8] = fm(inp["a_ffn_norm"][0])
    pvn[:, PV_KVN:PV_KVN + 8] = fm(inp["kv_norm"])
    pvn[:, PV_BN:PV_BN + 8] = fm(inp["b_norm"][0])
    pvn[:, PV_BFFN:PV_BFFN + 8] = fm(inp["b_ffn_norm"][0])
    pvn[:, PV_KN:PV_KN + 3] = np.asarray(inp["k_norm"], np.float32).T
    pvn[:, PV_QN:PV_QN + 3] = np.asarray(inp["b_q_norm"][0], np.float32).T
    invf = (500000.0 ** (-np.arange(0, 32, 2, dtype=np.float32) / 32)).astype(np.float32)
    pvn[0:32, PV_INVF] = np.concatenate([invf, invf])
    bseq, j = core // 4, core % 4
    for r in range(4):
        pvn[:, PV_SEL + r] = 1.0 if r < j else 0.0
        pvn[:, PV_SEL + 4 + r] = 1.0 if r == j - 1 else 0.0
    pvn[:, PV_HB] = -30000.0 if j == 0 else 0.0
    pvn[:, PV_EPS] = EPS
    pvn[:, PV_ONE] = 1.0
    return pvn


def kernel(**inputs):
    inp = {k: np.asarray(v) for k, v in inputs.items()}
    x = inp["x"].astype(np.float32, copy=False)
    cores = list(range(NCORES))
    xTs, xhs, poss = [], [], []
    for c in cores:
        bseq, j = c // 4, c % 4
        xc = x[bseq, j * S_CORE:(j + 1) * S_CORE, :]
        xTs.append(np.ascontiguousarray(xc.T))
        if j == 0:
            xhs.append(np.zeros((D, 4), np.float32))
        else:
            xhs.append(np.ascontiguousarray(x[bseq, j * S_CORE - 4:j * S_CORE, :].T))
        p = inp["positions"][bseq, j * S_CORE:(j + 1) * S_CORE].astype(np.int32)
        poss.append(np.ascontiguousarray(np.broadcast_to(p[None, :], (32, S_CORE))))
    pvecs = [make_pvec(inp, c) for c in cores]
    a_w_in = np.ascontiguousarray(inp["a_w_in"][0], np.float32)
    gaw = np.ascontiguousarray(np.transpose(inp["a_gate_a_w"][0], (1, 0, 2)), np.float32)
    gxw = np.ascontiguousarray(np.transpose(inp["a_gate_x_w"][0], (1, 0, 2)), np.float32)
    p0 = np.zeros((32, 32), np.float32)
    for m in range(16):
        p0[m + 16, m] = -1.0
    for m in range(16, 32):
        p0[m - 16, m] = 1.0
    li = np.arange(128)
    mown = np.ascontiguousarray(np.broadcast_to(np.where(li[:, None] <= li[None, :], 0.0, -30000.0)[:, None, :], (128, 4, 128))).astype(ml_dtypes.bfloat16)
    mprev = np.ascontiguousarray(np.broadcast_to(np.where(li[:, None] >= li[None, :], 0.0, -30000.0)[:, None, :], (128, 4, 128))).astype(ml_dtypes.bfloat16)
    ident = np.eye(128, dtype=np.float32).astype(ml_dtypes.bfloat16)

    nc = build_fused()
    w = lambda k: np.ascontiguousarray(inp[k][0], np.float32)
    shared = {"a_w_in": a_w_in, "gaw": gaw, "gxw": gxw, "a_w_out": w("a_w_out"), "a_ffn_w_in": w("a_ffn_w_in"),
              "a_ffn_w_out": w("a_ffn_w_out"), "w_kv": np.ascontiguousarray(inp["w_kv"], np.float32), "p0": p0,
              "b_w_q": w("b_w_q"), "b_w_o": w("b_w_o"), "b_ffn_w_in": w("b_ffn_w_in"), "b_ffn_w_out": w("b_ffn_w_out"),
              "mown": mown, "mprev": mprev, "ident": ident}
    in_maps = []
    for c in cores:
        m = dict(shared)
        m.update({"xT": xTs[c], "xh": xhs[c], "pvec": pvecs[c], "pos32": poss[c]})
        in_maps.append(m)
    res = run_bass_kernel_spmd(nc, in_maps, core_ids=cores).results
    out = np.empty((2, 4 * S_CORE, D), np.float32)
    for c in cores:
        bseq, j = c // 4, c % 4
        out[bseq, j * S_CORE:(j + 1) * S_CORE, :] = res[c]["outT"].T
    return out
```
